# Optimizing a Trainium2 kernel written in Bass

```python
import jax, jax.numpy as jnp
from jax import lax
import numpy as np

D_MODEL = 1024
BATCH = 8
SEQ = 2048
DEPTH = 1
DEC_BATCH = 128
DEC_SEQ = 8
PAST_LEN = 16384
PAGE_SIZE = 128

MIX_WIDTH = D_MODEL
GM_WIDTH = MIX_WIDTH // 2
ML_WIDTH = MIX_WIDTH - GM_WIDTH
GM_HEADS = 4
GM_HEAD_DIM = GM_WIDTH // GM_HEADS
ML_HEADS = 4
ML_HEAD_DIM = ML_WIDTH // ML_HEADS
CHUNK = 128
ML_CHUNK = 128
D_FF = 4 * D_MODEL
PLE_DIM = 256
EPS = 1e-6
SPLITS = (GM_WIDTH, 2 * GM_WIDTH, 2 * GM_WIDTH + ML_WIDTH, 2 * GM_WIDTH + 2 * ML_WIDTH,
          2 * GM_WIDTH + 3 * ML_WIDTH, 2 * GM_WIDTH + 4 * ML_WIDTH, 2 * GM_WIDTH + 4 * ML_WIDTH + ML_HEADS)
IN_COLS = 2 * GM_WIDTH + 4 * ML_WIDTH + 2 * ML_HEADS

kernel_name = "hymba_gmlp_mlstm_decoder_step"


def rmsnorm(x, g):
    xf = x.astype(jnp.float32)
    y = xf * lax.rsqrt(jnp.mean(xf * xf, axis=-1, keepdims=True) + EPS)
    return (y * g.astype(jnp.float32)).astype(x.dtype)


def layernorm(x, g, b):
    xf = x.astype(jnp.float32)
    mu = jnp.mean(xf, axis=-1, keepdims=True)
    var = jnp.mean(jnp.square(xf - mu), axis=-1, keepdims=True)
    y = (xf - mu) * lax.rsqrt(var + EPS)
    return (y * g.astype(jnp.float32) + b.astype(jnp.float32)).astype(x.dtype)


def _gmlp_mix(u, vn, w_s, b_s, L):
    B, T, _ = vn.shape
    n = T // L
    vh = vn.reshape(B, n, L, GM_HEADS, GM_HEAD_DIM)
    w = w_s[:, :L, :L]
    w = jnp.where(jnp.tril(jnp.ones((L, L), dtype=bool)), w, jnp.zeros_like(w))
    s = jnp.einsum('htj,bnjhd->bnthd', w, vh) + jnp.transpose(b_s[:, :L])[None, None, :, :, None]
    return u * s.reshape(B, T, GM_WIDTH)


def _mlstm_chunk(carry, inp):
    C, n, m = carry
    q, k, v, ig, lf = inp
    L = q.shape[2]
    b = jnp.cumsum(lf, axis=-1)
    g = b + m[..., None]
    D = b[..., :, None] - b[..., None, :] + ig[..., None, :]
    D = jnp.where(jnp.tril(jnp.ones((L, L), dtype=bool)), D, -jnp.inf)
    m_t = jnp.maximum(g, jnp.max(D, axis=-1))
    w_intra = jnp.exp(D - m_t[..., None])
    w_inter = jnp.exp(g - m_t)
    s = jnp.einsum('bhtd,bhsd->bhts', q, k) * w_intra
    num = jnp.einsum('bhts,bhse->bhte', s, v) + w_inter[..., None] * jnp.einsum('bhtd,bhde->bhte', q, C)
    den = jnp.sum(s, axis=-1) + w_inter * jnp.einsum('bhtd,bhd->bht', q, n)
    h = num / jnp.maximum(jnp.abs(den), jnp.exp(-m_t))[..., None]
    m_new = m_t[..., -1]
    w_end = jnp.exp(D[..., -1, :] - m_new[..., None])
    dec = jnp.exp(g[..., -1] - m_new)
    C_new = dec[..., None, None] * C + jnp.einsum('bhs,bhsd,bhse->bhde', w_end, k, v)
    n_new = dec[..., None] * n + jnp.einsum('bhs,bhsd->bhd', w_end, k)
    return (C_new, n_new, m_new), h


def _mlstm(q, k, v, ig, lf, state, L):
    B, T, _ = q.shape
    n = T // L

    def to_chunks(a):
        return a.astype(jnp.float32).reshape(B, n, L, ML_HEADS, ML_HEAD_DIM).transpose(1, 0, 3, 2, 4)

    def gate_chunks(a):
        return a.reshape(B, n, L, ML_HEADS).transpose(1, 0, 3, 2)

    xs = (to_chunks(q), to_chunks(k) * (ML_HEAD_DIM ** -0.5), to_chunks(v), gate_chunks(ig), gate_chunks(lf))
    state, h = lax.scan(_mlstm_chunk, state, xs)
    h = h.transpose(1, 0, 3, 2, 4).reshape(B, T, ML_WIDTH)
    return h, state


def _layer(h, pe, state, gm_len, ml_len, norm_mix, w_in, gm_ln_g, gm_ln_b, gm_ws, gm_bs,
           ml_b_i, ml_b_f, ml_norm, w_out, norm_ffn, w_up, w_down, norm_ple, w_ple_gate, w_ple_proj):
    B, T, _ = h.shape
    a = rmsnorm(h, norm_mix)
    z = a @ w_in
    u, vg, q, k, vm, o, zi, zf = jnp.split(z, SPLITS, axis=-1)
    u = jax.nn.gelu(u)
    vg = layernorm(jax.nn.gelu(vg), gm_ln_g, gm_ln_b)
    y_gm = _gmlp_mix(u, vg, gm_ws, gm_bs, gm_len)
    ig = (zi + ml_b_i).astype(jnp.float32)
    lf = jax.nn.log_sigmoid((zf + ml_b_f).astype(jnp.float32))
    hm, state = _mlstm(q, k, vm, ig, lf, state, ml_len)
    hm = rmsnorm(hm.reshape(B, T, ML_HEADS, ML_HEAD_DIM), ml_norm.reshape(ML_HEADS, ML_HEAD_DIM))
    y_ml = jax.nn.sigmoid(o) * hm.reshape(B, T, ML_WIDTH).astype(o.dtype)
    h = h + jnp.concatenate([y_gm, y_ml], axis=-1) @ w_out
    f = rmsnorm(h, norm_ffn) @ w_up
    h = h + jnp.square(jax.nn.relu(f)) @ w_down
    gate = jax.nn.sigmoid(rmsnorm(h, norm_ple) @ w_ple_gate)
    h = h + gate * (pe @ w_ple_proj)
    return h, state, vg


def setup_inputs(seed: int = 0) -> dict:
    key = jax.random.key(seed)
    ks = jax.random.split(key, 32)
    f32 = jnp.float32

    def nrm(k, shape, scale):
        return jax.random.normal(k, shape, f32) * scale

    def gain(k, shape):
        return 1.0 + 0.05 * jax.random.normal(k, shape, f32)

    f_bias = jnp.linspace(3.0, 6.0, ML_HEADS, dtype=f32)[None, :] + 0.1 * jax.random.normal(ks[11], (DEPTH, ML_HEADS), f32)
    return {
        "x_prompt": nrm(ks[0], (BATCH, SEQ, D_MODEL), 1.0),
        "x_sample": nrm(ks[1], (DEC_BATCH, DEC_SEQ, D_MODEL), 1.0),
        "p_prompt": nrm(ks[2], (DEPTH, BATCH, SEQ, PLE_DIM), 1.0),
        "p_sample": nrm(ks[3], (DEPTH, DEC_BATCH, DEC_SEQ, PLE_DIM), 1.0),
        "state_C": nrm(ks[4], (DEPTH, DEC_BATCH, ML_HEADS, ML_HEAD_DIM, ML_HEAD_DIM), 0.1),
        "state_n": jnp.abs(nrm(ks[5], (DEPTH, DEC_BATCH, ML_HEADS, ML_HEAD_DIM), 0.1)),
        "state_m": nrm(ks[6], (DEPTH, DEC_BATCH, ML_HEADS), 1.0),
        "norm_mix": gain(ks[7], (DEPTH, D_MODEL)),
        "w_in": nrm(ks[8], (DEPTH, D_MODEL, IN_COLS), D_MODEL ** -0.5),
        "gm_ln_g": gain(ks[9], (DEPTH, GM_WIDTH)),
        "gm_ln_b": nrm(ks[10], (DEPTH, GM_WIDTH), 0.02),
        "gm_ws": nrm(ks[12], (DEPTH, GM_HEADS, CHUNK, CHUNK), 0.5 * CHUNK ** -0.5),
        "gm_bs": gain(ks[13], (DEPTH, GM_HEADS, CHUNK)),
        "ml_b_i": nrm(ks[14], (DEPTH, ML_HEADS), 0.1),
        "ml_b_f": f_bias,
        "ml_norm": gain(ks[15], (DEPTH, ML_WIDTH)),
        "w_out": nrm(ks[16], (DEPTH, MIX_WIDTH, D_MODEL), MIX_WIDTH ** -0.5),
        "norm_ffn": gain(ks[17], (DEPTH, D_MODEL)),
        "w_up": nrm(ks[18], (DEPTH, D_MODEL, D_FF), D_MODEL ** -0.5),
        "w_down": nrm(ks[19], (DEPTH, D_FF, D_MODEL), D_FF ** -0.5),
        "norm_ple": gain(ks[20], (DEPTH, D_MODEL)),
        "w_ple_gate": nrm(ks[21], (DEPTH, D_MODEL, D_MODEL), D_MODEL ** -0.5),
        "w_ple_proj": nrm(ks[22], (DEPTH, PLE_DIM, D_MODEL), PLE_DIM ** -0.5),
        "norm_final": gain(ks[23], (D_MODEL,)),
    }


def reference(x_prompt, x_sample, p_prompt, p_sample, state_C, state_n, state_m,
              norm_mix, w_in, gm_ln_g, gm_ln_b, gm_ws, gm_bs, ml_b_i, ml_b_f, ml_norm,
              w_out, norm_ffn, w_up, w_down, norm_ple, w_ple_gate, w_ple_proj, norm_final):
    hp, hs = x_prompt, x_sample
    B = x_prompt.shape[0]
    Cp, Np, Mp, Vp, Cs, Ns, Ms, Vs = [], [], [], [], [], [], [], []
    for l in range(DEPTH):
        w = (norm_mix[l], w_in[l], gm_ln_g[l], gm_ln_b[l], gm_ws[l], gm_bs[l], ml_b_i[l], ml_b_f[l],
             ml_norm[l], w_out[l], norm_ffn[l], w_up[l], w_down[l], norm_ple[l], w_ple_gate[l], w_ple_proj[l])
        st0 = (jnp.zeros((B, ML_HEADS, ML_HEAD_DIM, ML_HEAD_DIM), jnp.float32),
               jnp.zeros((B, ML_HEADS, ML_HEAD_DIM), jnp.float32),
               jnp.zeros((B, ML_HEADS), jnp.float32))
        hp, (c, n, m), vg = _layer(hp, p_prompt[l], st0, CHUNK, ML_CHUNK, *w)
        Cp.append(c); Np.append(n); Mp.append(m); Vp.append(vg[:, -CHUNK:])
        sts = (state_C[l].astype(jnp.float32), state_n[l].astype(jnp.float32), state_m[l].astype(jnp.float32))
        hs, (c, n, m), vg = _layer(hs, p_sample[l], sts, DEC_SEQ, DEC_SEQ, *w)
        Cs.append(c); Ns.append(n); Ms.append(m); Vs.append(vg)
    y_prompt = rmsnorm(hp, norm_final)
    y_sample = rmsnorm(hs, norm_final)
    return (y_prompt, y_sample, jnp.stack(Cp), jnp.stack(Np), jnp.stack(Mp), jnp.stack(Vp),
            jnp.stack(Cs), jnp.stack(Ns), jnp.stack(Ms), jnp.stack(Vs))
```

```python
import numpy as np
import concourse.bass as bass
import concourse.mybir as mybir
from concourse.bass_utils import run_bass_kernel_spmd

F32 = mybir.dt.float32
BF16 = mybir.dt.bfloat16
AF = mybir.ActivationFunctionType
ALU = mybir.AluOpType
AX = mybir.AxisListType

EPOCH = 30000
NDMASEM = 12
NEG = -30000.0
EPS = 1e-6
NCORES = 8


def _dsize(dt):
    return mybir.dt.size(dt)


class Tok:
    __slots__ = ("key", "val", "clock", "fin")

    def __init__(self, key, val, clock):
        self.key, self.val, self.clock = key, val, clock
        self.fin = 0.0


def _nelem(ap):
    n = 1
    for (_s, c) in list(ap.ap)[1:]:
        n *= c
    return n


SYNC_LAT = 0.35
PE_WAIT_MARGIN = 2.0


class Sched:
    ENG = ("pe", "act", "dve", "pool", "sp")

    def __init__(self, nc):
        self.nc = nc
        self.prog = {e: [] for e in self.ENG}
        self.count = {e: 0 for e in self.ENG}
        self.knows = {e: {} for e in self.ENG}
        self.sems = {}
        self.dma_i = {e: 0 for e in self.ENG}
        self.dma_cnt = {}
        self.dma_last = {}
        self.bufs = {}
        self._ctx = []
        self.nwaits = 0
        self.nops = 0
        self.marks = []
        self.capture = None
        self.free = {e: 0.0 for e in self.ENG}

    def _cost(self, eng, writes, reads):
        ap = writes[0] if writes else (reads[0] if reads else None)
        n = _nelem(ap) if ap is not None else 1
        if reads:
            n = max(n, max(_nelem(r) for r in reads) if eng != "pe" else n)
        if eng == "dve":
            return 0.06 + n * 0.00125
        if eng == "act":
            return 0.2 + n * 0.0008
        if eng == "pool":
            return 0.1 + n * 0.002
        return 0.1

    def _est_start(self, eng, deps):
        t = self.free[eng]
        for d in deps:
            f = d.fin + (0.0 if d.key == ("e", eng) else SYNC_LAT)
            if f > t:
                t = f
        return t

    def run_streams(self, streams):
        ptr = [0] * len(streams)
        while True:
            best = None
            for i, stm in enumerate(streams):
                if ptr[i] >= len(stm):
                    continue
                kind, eng, a, b, c, cost = stm[ptr[i]]
                if kind == "op":
                    deps, _, _ = self._deps_for(b, c, eng)
                else:
                    deps, _, _ = self._deps_for([b], [a], "dma")
                st_ = self._est_start(eng, deps)
                if eng == "pe" and st_ > self.free[eng] + 1e-9:
                    st_ += PE_WAIT_MARGIN
                if best is None or (st_, i) < best[0]:
                    best = ((st_, i), i)
            if best is None:
                break
            i = best[1]
            kind, eng, a, b, c, cost = streams[i][ptr[i]]
            ptr[i] += 1
            if kind == "op":
                self.op(eng, a, b, c, cost=cost)
            else:
                self.dma(a, b, eng=eng, **c)

    def _sem(self, key):
        s = self.sems.get(key)
        if s is None:
            cm = self.nc.semaphore("s_" + "_".join(str(k) for k in key))
            s = cm.__enter__()
            self._ctx.append(cm)
            self.sems[key] = s
        return s

    def _eng_sem_val(self, eng, v):
        ep = (v - 1) // EPOCH
        return self._sem((eng, ep)), v - ep * EPOCH

    @staticmethod
    def _intervals(pairs, limit=64):
        pairs = [(abs(st), c) for (st, c) in pairs if c > 1]
        if not pairs:
            return [(0, 1)]
        pairs.sort(key=lambda p: -p[0])
        out = [(0, 0)]
        res = []

        def rec(base, idx, budget):
            rest = pairs[idx:]
            ext = sum((c - 1) * st for st, c in rest) + 1
            if not rest:
                return [(base, base + 1)]
            st0, c0 = rest[0]
            inner = sum((c - 1) * st for st, c in rest[1:]) + 1
            if st0 > inner and c0 <= budget:
                r = []
                for i in range(c0):
                    r += rec(base + i * st0, idx + 1, max(1, budget // c0))
                return r
            return [(base, base + ext)]
        return rec(0, 0, limit)

    @staticmethod
    def regions(ap):
        t = ap.tensor
        name = t.name
        esz = _dsize(ap.dtype)
        pairs = [tuple(p) for p in ap.ap]
        tn = type(t).__name__
        if not ("SB" in tn or "PSum" in tn):
            return [(name, 0, 1, (ap.offset + lo) * esz, (ap.offset + hi) * esz)
                    for (lo, hi) in Sched._intervals(pairs)]
        prow = 1
        for d in list(t.shape)[1:]:
            prow *= d
        prow_b = prow * _dsize(t.dtype)
        if "PSum" in tn:
            return [("~" + name, 0, 128, 0, prow_b)]
        off_b = ap.offset * esz
        p0 = off_b // prow_b
        lo0 = off_b - p0 * prow_b
        pstep, pcnt = pairs[0]
        if pcnt > 1:
            pst = (pstep * esz) // prow_b
            p1 = p0 + (pcnt - 1) * max(pst, 1) + 1
        else:
            p1 = p0 + 1
        return [(name, p0, p1, (lo0 + lo * esz) // 4 * 4, (lo0 + hi * esz + 3) // 4 * 4)
                for (lo, hi) in Sched._intervals(pairs[1:])]

    def _deps_for(self, reads, writes, eng=None):
        deps = []
        rr = [r for a in reads for r in self.regions(a)]
        wr = [r for a in writes for r in self.regions(a)]
        for (name, p0, p1, lo, hi) in rr:
            b = self.bufs.get(name)
            if b is None:
                continue
            if name[0] == "~":
                for (q0, q1, l2, h2, tok) in b["w"]:
                    e2, w2 = b["meta"]
                    if e2 == eng and not w2:
                        continue
                    deps.append(tok)
                continue
            for (q0, q1, l2, h2, tok) in b["w"]:
                if lo < h2 and l2 < hi and p0 < q1 and q0 < p1:
                    deps.append(tok)
        for (name, p0, p1, lo, hi) in wr:
            b = self.bufs.get(name)
            if b is None:
                continue
            if name[0] == "~":
                for (q0, q1, l2, h2, tok) in b["w"]:
                    e2, w2 = b["meta"]
                    if e2 == eng and eng == "pe":
                        continue
                    deps.append(tok)
                continue
            for (q0, q1, l2, h2, tok) in b["w"]:
                if lo < h2 and l2 < hi and p0 < q1 and q0 < p1:
                    deps.append(tok)
            for (q0, q1, l2, h2, tok) in b["r"]:
                if lo < h2 and l2 < hi and p0 < q1 and q0 < p1:
                    deps.append(tok)
        return deps, rr, wr

    def _record(self, rr, wr, tok, eng=None):
        for (name, p0, p1, lo, hi) in wr:
            b = self.bufs.setdefault(name, {"w": [], "r": []})
            if name[0] == "~":
                b["w"] = [(p0, p1, lo, hi, tok)]
                b["meta"] = (eng, True)
                continue
            b["w"] = [e for e in b["w"] if not (p0 <= e[0] and e[1] <= p1 and lo <= e[2] and e[3] <= hi)]
            b["r"] = [e for e in b["r"] if not (p0 <= e[0] and e[1] <= p1 and lo <= e[2] and e[3] <= hi)]
            b["w"].append((p0, p1, lo, hi, tok))
        for (name, p0, p1, lo, hi) in rr:
            b = self.bufs.setdefault(name, {"w": [], "r": []})
            if name[0] == "~":
                b["w"] = [(p0, p1, lo, hi, tok)]
                b["meta"] = (eng, False)
                continue
            b["r"] = [e for e in b["r"] if not (e[0] == p0 and e[1] == p1 and e[2] == lo and e[3] == hi
                                                and e[4].key == tok.key and e[4].val <= tok.val)]
            b["r"].append((p0, p1, lo, hi, tok))

    def _waits(self, eng, deps):
        kn = self.knows[eng]
        need = {}
        for t in deps:
            if kn.get(t.key, 0) >= t.val:
                continue
            cur = need.get(t.key)
            if cur is None or cur.val < t.val:
                need[t.key] = t
        out = []
        for t in sorted(need.values(), key=lambda t: -len(t.clock)):
            if kn.get(t.key, 0) >= t.val:
                continue
            out.append(t)
            for k, v in t.clock.items():
                if kn.get(k, 0) < v:
                    kn[k] = v
        return out

    def _emit_waits(self, eng, toks):
        for t in toks:
            self.nwaits += 1
            if t.key[0] == "d":
                sem, val = self._sem(t.key), t.val
            else:
                sem, val = self._eng_sem_val(t.key[1], t.val)
            self.prog[eng].append(lambda e, sem=sem, val=val: e.wait_ge(sem, val))

    def op(self, eng, fn, reads, writes, cost=None):
        if cost is None:
            cost = self._cost(eng, writes, reads)
        if self.capture is not None:
            self.capture.append(("op", eng, fn, reads, writes, cost))
            return None
        self.nops += 1
        deps, rr, wr = self._deps_for(reads, writes, eng)
        start = self._est_start(eng, deps)
        w = self._waits(eng, deps)
        self._emit_waits(eng, w)
        self.count[eng] += 1
        v = self.count[eng]
        sem, _lv = self._eng_sem_val(eng, v)
        key = ("e", eng)
        clock = dict(self.knows[eng])
        clock[key] = v
        tok = Tok(key, v, clock)
        tok.fin = start + cost
        self.free[eng] = start + cost
        self.prog[eng].append(lambda e, fn=fn, sem=sem: fn(e).then_inc(sem, 1))
        self._record(rr, wr, tok, eng)
        return tok

    def dma(self, out, in_, eng="sp", **kw):
        if self.capture is not None:
            self.capture.append(("dma", eng, out, in_, kw, 0.0))
            return None
        self.nops += 1
        deps, rr, wr = self._deps_for([in_], [out], "dma")
        start = self._est_start(eng, deps)
        nbytes = _nelem(out) * 128 * _dsize(out.dtype)
        self.free[eng] = start + (0.6 if eng == "pool" else 0.08)
        i = self.dma_i[eng]
        self.dma_i[eng] += 1
        skey = ("d", eng, i % NDMASEM)
        prev = self.dma_last.get(skey)
        if prev is not None:
            deps.append(prev)
        w = self._waits(eng, deps)
        self._emit_waits(eng, w)
        val = self.dma_cnt.get(skey, 0) + 16
        self.dma_cnt[skey] = val
        sem = self._sem(skey)
        clock = dict(self.knows[eng])
        clock[skey] = val
        tok = Tok(skey, val, clock)
        tok.fin = start + 2.0 + nbytes / 80e3
        self.dma_last[skey] = tok
        self.prog[eng].append(
            lambda e, out=out, in_=in_, sem=sem, kw=kw: e.dma_start(out=out, in_=in_, **kw).then_inc(sem, 16))
        self._record(rr, wr, tok, "dma")
        return tok

    def mm(self, out, pairs):
        n = len(pairs)
        reads = []
        for (l, r) in pairs:
            reads += [l, r]

        def fn(e):
            ins = None
            for i, (l, r) in enumerate(pairs):
                ins = e.matmul(out, l, r, start=(i == 0), stop=(i == n - 1))
            return ins
        f32 = 8.0 if _dsize(pairs[0][1].dtype) == 4 else 1.0
        cost = sum((max(_nelem(r), 128) / 2100.0 + 0.01) * f32 for (_l, r) in pairs)
        return self.op("pe", fn, reads, [out], cost=cost)

    def tr(self, out, in_, ident):
        return self.op("pe", lambda e: e.transpose(out, in_, ident), [in_, ident], [out], cost=0.14)

    def finish(self):
        w = self._waits("sp", list(self.dma_last.values()))
        self._emit_waits("sp", w)
        nc = self.nc
        prog = self.prog
        with nc.Block() as block:
            @block.tensor
            def _(e):
                for f in prog["pe"]:
                    f(e)

            @block.scalar
            def _(e):
                for f in prog["act"]:
                    f(e)

            @block.vector
            def _(e):
                for f in prog["dve"]:
                    f(e)

            @block.gpsimd
            def _(e):
                for f in prog["pool"]:
                    f(e)

            @block.sync
            def _(e):
                for f in prog["sp"]:
                    f(e)
        for cm in reversed(self._ctx):
            cm.__exit__(None, None, None)
        self._ctx = []


DM = 1024
INC = 3080
DFF = 4096
PLE = 256
NPT = 16
ARENA_F32 = 37376
CM_IDENT, CM_ONES, CM_CUM, CM_MASK, CM_MASKT, CM_SEL, CM_01 = 0, 1, 2, 4, 6, 8, 10
RV_LNG, RV_LNB, RV_BI, RV_BF, RV_MLG = 0, 512, 1024, 1028, 1032
RV_N = RV_MLG
GN_MIX, GN_FFN, GN_PLE, GN_BS = 0, 8, 16, 24


def build():
    nc = bass.Bass("TRN2", target_bir_lowering=False)

    def din(name, shape):
        return nc.dram_tensor(name, shape, F32, kind="ExternalInput").ap()

    def dout(name, shape):
        return nc.dram_tensor(name, shape, F32, kind="ExternalOutput").ap()

    xp = din("xp", [2048, DM]); xs = din("xs", [128, DM])
    pp = din("pp", [2048, PLE]); psm = din("psm", [128, PLE])
    sC = din("sC", [16, 4, 128, 128]); snT = din("snT", [4, 128, 16]); smt = din("smt", [128, 4])
    w_in = din("w_in", [DM, INC]); w_out = din("w_out", [DM, DM])
    w_up = din("w_up", [DM, DFF]); w_down = din("w_down", [DFF, DM])
    w_pg = din("w_pg", [DM, DM]); w_pp = din("w_pp", [PLE, DM])
    gains = din("gains", [128, 32]); rowv = din("rowv", [RV_N]); nfin = din("nfin", [DM]); mlg = din("mlg", [512])
    wsg = din("wsg", [8, 128, 128])
    cmat = din("cmat", [12, 128, 128]); csm = din("csm", [128, 32])

    y_p = dout("y_p", [2048, DM]); y_s = dout("y_s", [128, DM])
    o_Cp = dout("o_Cp", [4, 128, 128]); o_Np = dout("o_Np", [4, 128]); o_Mp = dout("o_Mp", [1, 4])
    o_Vp = dout("o_Vp", [128, 512])
    o_Cs = dout("o_Cs", [16, 4, 128, 128]); o_NsT = dout("o_NsT", [4, 128, 16]); o_Ms = dout("o_Ms", [16, 4])
    o_Vs = dout("o_Vs", [128, 512])

    S = Sched(nc)
    from contextlib import ExitStack
    with ExitStack() as st:
        def sb(name, shape, dt=F32):
            return st.enter_context(nc.sbuf_tensor(name, shape, dt))

        def aps(*xs_):
            return [x for x in xs_ if not isinstance(x, (int, float)) and x is not None]

        def act(out, in_, func, bias=None, scale=None, accum=None):
            kw = {}
            if bias is not None:
                kw["bias"] = bias
            if scale is not None:
                kw["scale"] = scale
            if accum is not None:
                kw["accum_out"] = accum
            return S.op("act", lambda e: e.activation(out=out, in_=in_, func=func, **kw),
                        aps(in_, bias, scale), aps(out, accum))

        def tt(eng, out, in0, in1, op):
            return S.op(eng, lambda e: e.tensor_tensor(out=out, in0=in0, in1=in1, op=op), [in0, in1], [out])

        def ts(eng, out, in0, s1, op0, s2=None, op1=None, accum=None):
            kw = {}
            if op1 is not None:
                kw["op1"] = op1
            if accum is not None:
                kw["accum_out"] = accum
            return S.op(eng, lambda e: e.tensor_scalar(out=out, in0=in0, scalar1=s1, scalar2=s2, op0=op0, **kw),
                        aps(in0, s1, s2), aps(out, accum))

        def stt(eng, out, in0, scalar, in1, op0, op1, accum=None):
            kw = {}
            if accum is not None:
                kw["accum_out"] = accum
            return S.op(eng, lambda e: e.scalar_tensor_tensor(out=out, in0=in0, scalar=scalar, in1=in1,
                                                              op0=op0, op1=op1, **kw),
                        aps(in0, scalar, in1), aps(out, accum))

        def cp(eng, out, in_):
            if eng == "act":
                return act(out, in_, AF.Copy)
            return S.op(eng, lambda e: e.tensor_copy(out=out, in_=in_), [in_], [out])

        def bnstats(out, in_):
            return S.op("dve", lambda e: e.bn_stats(out=out, in_=in_), [in_], [out])

        def bnaggr(out, in_):
            return S.op("dve", lambda e: e.bn_aggr(out=out, in_=in_), [in_], [out])

        def redmax(out, in_):
            return S.op("dve", lambda e: e.tensor_reduce(out=out, in_=in_, axis=AX.X, op=ALU.max), [in_], [out])

        def recip(out, in_):
            return S.op("dve", lambda e: e.reciprocal(out=out, in_=in_), [in_], [out])

        def memset(eng, out, val):
            return S.op(eng, lambda e: e.memset(out, val), [], [out])

        CM = sb("CM", [128, 12, 128])
        CS = sb("CS", [128, 32])
        GN = sb("GN", [128, 32])
        MLGH = sb("MLGH", [128, 512])
        RV = sb("RV", [128, RV_N])
        IDB = sb("IDB", [128, 128], BF16)
        ONESB = sb("ONESB", [128, 128], BF16)
        SMALLB = sb("SMALLB", [128, 6, 32], BF16)
        WM = sb("WM", [128, 8, 128], BF16)
        NEGH = sb("NEGH", [128, 8])
        SMT = sb("SMT", [128, 4])
        MRUN = sb("MRUN", [128, 4])
        CST = sb("CST", [128, 4, 129])
        CSTB = sb("CSTB", [128, 4, 129], BF16)
        KTOKS = sb("KTOKS", [128, 4, 128], BF16)
        VWS = sb("VWS", [128, 4, 130], BF16)
        DECB = sb("DECBS", [128, 16, 4])
        NCOL = sb("NCOLS", [128, 4, 16])
        H = sb("H", [128, 9, DM])
        SMALL = sb("SMALL", [128, 6, 160])
        AR = sb("AR", [128, ARENA_F32])
        ARB = AR.bitcast(BF16)
        PB = [st.enter_context(nc.psum_tensor(f"pb{i}", [128, 512], F32)) for i in range(8)]
        PBB = [p.bitcast(BF16) for p in PB]

        class Arena:
            def __init__(self, off=0):
                self.off = off

            def f32(self, n, shape=None):
                assert self.off % 4 == 0
                o = self.off // 4
                self.off += n * 4
                assert self.off <= ARENA_F32 * 4, ("arena overflow", self.off)
                v = AR[:, o:o + n]
                return v if shape is None else v.rearrange(shape[0], **shape[1])

            def bf(self, n, shape=None):
                o = self.off // 2
                self.off += ((n * 2 + 3) // 4) * 4
                assert self.off <= ARENA_F32 * 4, ("arena overflow", self.off)
                v = ARB[:, o:o + n]
                return v if shape is None else v.rearrange(shape[0], **shape[1])

        class Banks:
            def __init__(self, ids):
                self.ids = list(ids)
                self.i = 0

            def next(self):
                b = self.ids[self.i % len(self.ids)]
                self.i += 1
                return b

        S.dma(H[:, 0, :], xp[0:128, :])
        S.dma(CM[:].rearrange("p k n -> p (k n)").rearrange("p (k n) -> p k n", k=12), cmat.rearrange("k p n -> p k n"))
        S.dma(CS[:], csm)
        S.dma(GN[:], gains)
        S.dma(RV[:], rowv.partition_broadcast(128))
        S.dma(SMT[:], smt)
        memset("pool", NEGH[:], -0.5)
        S.dma(MLGH[:], mlg.partition_broadcast(128))
        ts("dve", MLGH[:], MLGH[:], 0.5, ALU.mult)
        cp("dve", IDB[:], CM[:, CM_IDENT, :])
        cp("dve", ONESB[:], CM[:, CM_ONES, :])
        IDF = CM[:, CM_IDENT, :]
        ONESF = CM[:, CM_ONES, :]
        memset("pool", MRUN[:], 0.0)
        memset("pool", CST[:], 0.0)
        memset("pool", CSTB[:], 0.0)
        A0 = Arena(64 * 1024)
        wtmp = A0.f32(8 * 128, ("p (k n) -> p k n", dict(k=8)))
        wtb = A0.bf(8 * 128, ("p (k n) -> p k n", dict(k=8)))
        S.dma(wtmp, wsg.rearrange("k p n -> p k n"))
        for ty in range(2):
            tt("dve", wtb[:, ty * 4:(ty + 1) * 4, :], wtmp[:, ty * 4:(ty + 1) * 4, :],
               CM[:, CM_01 + ty, :].unsqueeze(1).to_broadcast([128, 4, 128]), ALU.mult)
        for k in range(8):
            S.tr(PBB[0][:, k * 128:(k + 1) * 128], wtb[:, k, :], IDB[:])
        cp("dve", WM[:], PBB[0][:, 0:1024].rearrange("p (k n) -> p k n", k=8))

        def load_cast(dst_bf, src_dram):
            S.dma(dst_bf, src_dram, eng="pool")

        def small(d, off, n):
            return SMALL[:, d, off:off + n]

        def bcast_rows(d, soff, vec, dg, ps_out):
            hl = SMALLB[:, d, soff:soff + 8]
            cp("dve", hl[:, 0:4], vec)
            tt("dve", hl[:, 4:8], vec, hl[:, 0:4], ALU.subtract)
            tt("pool", dg, IDB[:].unsqueeze(1).to_broadcast([128, 8, 128]),
               hl.unsqueeze(2).to_broadcast([128, 8, 128]), ALU.mult)
            S.mm(ps_out, [(ONESB[:], dg[:, 0:4, :].rearrange("p h s -> p (h s)")),
                          (ONESB[:], dg[:, 4:8, :].rearrange("p h s -> p (h s)"))])

        def rstd_from_ss(tmp, ss, n, inv_d, out):
            ts("pool", tmp, ss, inv_d, ALU.mult, EPS, ALU.add)
            tt("pool", out, tmp, NEGH[:, 0:n], ALU.pow)

        def norm_T(d, hsrc, a_bf, aT_dst, ssoff, goff, bank):
            ss = small(d, ssoff, 1)
            rs = small(d, ssoff + 1, 1)
            tmp = small(d, ssoff + 2, 1)
            memset("pool", ss, 0.0)
            act(a_bf, hsrc, AF.Square, accum=ss)
            rstd_from_ss(tmp, ss, 1, 1.0 / DM, rs)
            act(a_bf, hsrc, AF.Copy, scale=rs)
            for kc in range(8):
                S.tr(PBB[bank][:, kc * 128:(kc + 1) * 128], a_bf[:, kc * 128:(kc + 1) * 128], IDB[:])
            tt("dve", aT_dst, PBB[bank][:, 0:1024].rearrange("p (k t) -> p k t", k=8),
               GN[:, goff:goff + 8].unsqueeze(2).to_broadcast([128, 8, 128]), ALU.mult)

        def run_pipeline(tiles, stages, background=None):
            n = len(tiles)
            for it in range(n + len(stages) - 1):
                streams = []
                for s_i in reversed(range(len(stages))):
                    ti = it - s_i
                    if 0 <= ti < n:
                        S.capture = []
                        for _ in stages[s_i](tiles[ti]):
                            pass
                        streams.append(S.capture)
                        S.capture = None
                busy = {}
                for stm in streams:
                    for o in stm:
                        busy[o[1]] = busy.get(o[1], 0.0) + o[5]
                if background is not None and it in background and background[it] is not None:
                    streams.append(background[it])
                S.run_streams(streams)
                S.marks.append(("it", it, dict(S.free), busy))

        def ptile(i):
            return dict(ty=0, i=i, x=xp[i * 128:(i + 1) * 128, :], p=pp[i * 128:(i + 1) * 128, :],
                        y=y_p[i * 128:(i + 1) * 128, :])
        stile = dict(ty=1, i=0, x=xs, p=psm, y=y_s)
        SGS = [[ptile(i) for i in range(4)] + [stile] + [ptile(i) for i in range(4, 8)],
               [ptile(i) for i in range(8, 16)]]
        gidx = [0]

        R = Arena()
        R1_OFF = R.off
        WG = R.bf(8 * DM, ("p (k n) -> p k n", dict(k=8)))
        WP = R.bf(2 * DM, ("p (k n) -> p k n", dict(k=2)))
        NF = R.f32(DM)
        R2_OFF = R.off

        WA_OFF = ARENA_F32 * 4 - (8 * INC + 8 * DM) * 2
        WA = Arena(WA_OFF)
        WIN = WA.bf(8 * INC, ("p (k n) -> p k n", dict(k=8)))
        WOUT = WA.bf(8 * DM, ("p (k n) -> p k n", dict(k=8)))
        COLG = ((0, 512), (512, 512), (2560, 512), (1024, 512), (1536, 512), (2048, 512), (3072, 8))

        BWA = Arena(WA_OFF)
        WU = [BWA.bf(8 * 1024, ("p (k n) -> p k n", dict(k=8))), None]
        WD = [BWA.bf(8 * 1024, ("p (k n) -> p k n", dict(k=8))), None]
        WU[1] = BWA.bf(8 * 1024, ("p (k n) -> p k n", dict(k=8)))
        assert BWA.off <= WA_OFF + 8 * INC * 2
        WD[1] = Arena(R2_OFF + (8 * 9 * 128 + DM + 2 * 8 * 512) * 2).bf(8 * 1024, ("p (k n) -> p k n", dict(k=8)))

        def load_q(qd):
            wu, wd = WU[qd % 2], WD[qd % 2]
            w_up_v = w_up.rearrange("(k p) n -> p k n", p=128)
            for hf in range(2):
                c0 = qd * 1024 + hf * 512
                load_cast(wu[:, :, hf * 512:(hf + 1) * 512], w_up_v[:, :, c0:c0 + 512])
            for hf in range(2):
                r0 = qd * 1024 + hf * 512
                load_cast(wd[:, hf * 4:(hf + 1) * 4, :],
                          w_down[r0:r0 + 512, :].rearrange("(f p) n -> p f n", p=128))

        def load_phaseA_weights(by_group):
            S.capture = []
            w_in_v = w_in.rearrange("(k p) n -> p k n", p=128)
            for gi_, (c0, n) in enumerate(COLG):
                load_cast(WIN[:, :, c0:c0 + n], w_in_v[:, :, c0:c0 + n])
                if by_group and gi_ == 2:
                    for slot_i in range(1, 9):
                        S.dma(H[:, slot_i, :], SGS[0][slot_i]["x"], eng="pool")
            load_cast(WOUT, w_out.rearrange("(k p) n -> p k n", p=128))
            stream = S.capture
            S.capture = None
            return stream

        for sg_i, sg in enumerate(SGS):
            for slot_i, t in enumerate(sg):
                t["slot"] = slot_i
                t["g"] = gidx[0]
                gidx[0] += 1
                if sg_i > 0:
                    S.dma(H[:, slot_i, :], t["x"])

            A = Arena(0)
            AT = [A.bf(8 * 128, ("p (k n) -> p k n", dict(k=8))) for _ in range(2)]
            A_BF = A.bf(DM)
            U = [A.bf(512) for _ in range(2)]
            VN = [A.f32(512) for _ in range(2)]
            VNB = A.bf(512)
            TH = A.f32(512)
            QTOK = [A.bf(512) for _ in range(2)]
            YGM = A.bf(512)
            DGA = A.bf(1024, ("p (h s) -> p h s", dict(h=8)))
            PBIG = A.f32(512, ("p (h s) -> p h s", dict(h=4)))
            QT = [A.bf(512, ("p (h t) -> p h t", dict(h=4))) for _ in range(3)]
            KT = [A.bf(512, ("p (h t) -> p h t", dict(h=4))) for _ in range(3)]
            KTOK = [A.bf(512, ("p (h d) -> p h d", dict(h=4))) for _ in range(4)]
            VAUG = [A.bf(4 * 130, ("p (h e) -> p h e", dict(h=4)))[:, :, 0:129] for _ in range(4)]
            SGM = [A.bf(512, ("p (h e) -> p h e", dict(h=4))) for _ in range(5)]
            YT = [A.bf(8 * 128, ("p (k t) -> p k t", dict(k=8))) for _ in range(4)]
            NUM = [A.f32(4 * 129, ("p (h e) -> p h e", dict(h=4))) for _ in range(2)]
            DGC = A.bf(1024, ("p (h s) -> p h s", dict(h=8)))
            BIG1 = A.f32(512, ("p (h s) -> p h s", dict(h=4)))
            STB = A.bf(512, ("p (h s) -> p h s", dict(h=4)))
            QCS = A.f32(4 * 129, ("p (h e) -> p h e", dict(h=4)))
            VW = A.bf(4 * 130, ("p (h e) -> p h e", dict(h=4)))[:, :, 0:129]
            YML = A.bf(512)
            JUNK = A.f32(128)
            has_s = any(t["ty"] == 1 for t in sg)
            if has_s:
                CSB3 = [A.bf(16 * 130, ("p (j e) -> p j e", dict(j=16)))[:, :, 0:129] for _ in range(2)]
                QTD = A.bf(16 * 128, ("p (j t) -> p j t", dict(j=16)))
                DECS = A.f32(64, ("p (j h) -> p j h", dict(j=16)))
            assert A.off <= WA_OFF, ("phase A work buffers overflow into weights", A.off, WA_OFF)
            bgA = load_phaseA_weights(True) if sg_i == 0 else None
            if has_s:
                memset("pool", QTD, 0.0)
            for v_ in VAUG:
                memset("pool", v_[:, :, 128:129], 1.0)

            bP = Banks([1, 2])
            bG = Banks([3])
            bP1 = Banks([4])
            bM1 = Banks([5, 6])
            bM2 = Banks([7])

            def hv(bx, by, h):
                b_ = PB[bx] if h % 2 == 0 else PB[by]
                return b_[:, (h // 2) * 129:(h // 2) * 129 + 129]

            def stage_N(t):
                g = t["g"]
                norm_T(g % 6, H[:, t["slot"], :], A_BF, AT[g % 2], 0, GN_MIX, 0)
                yield

            def stage_P(t):
                ty = t["ty"]; g = t["g"]; d2 = g % 2; d3 = g % 3; d4 = g % 4
                sm = lambda off, n: small(g % 6, off, n)
                at = AT[d2]
                lhs = [at[:, kc, :] for kc in range(8)]

                def proj(col0, n, bank):
                    ps = PB[bank][:, 0:n]
                    S.mm(ps, [(lhs[kc], WIN[:, kc, col0:col0 + n]) for kc in range(8)])
                    return ps
                ps_u = proj(0, 512, bP.next())
                act(U[d2], ps_u, AF.Gelu_apprx_tanh)
                ps_v = proj(512, 512, bP.next())
                act(VN[d2], ps_v, AF.Gelu_apprx_tanh)
                ps_o = proj(2560, 512, bP.next())
                act(TH, ps_o, AF.Tanh, scale=0.5)
                yield
                stt("dve", SGM[g % 5], TH.rearrange("p (h e) -> p h e", h=4), 1.0,
                    MLGH[:].rearrange("p (h e) -> p h e", h=4), ALU.add, ALU.mult)
                ps_q = proj(1024, 512, bP.next())
                cp("act", QTOK[d2], ps_q)
                yield
                ps_k = proj(1536, 512, bP.next())
                act(KTOK[g % 4], ps_k.rearrange("p (h d) -> p h d", h=4), AF.Copy, scale=float(128 ** -0.5))
                yield
                ps_vm = proj(2048, 512, bP.next())
                cp("dve", VAUG[g % 4][:, :, 0:128], ps_vm.rearrange("p (h d) -> p h d", h=4))
                yield
                bg = bP.next()
                ps_g = proj(3072, 8, bg)
                zg = sm(4, 8)
                cp("dve", zg, ps_g)
                yield

            def stage_P1(t):
                ty = t["ty"]; g = t["g"]; d2 = g % 2; d3 = g % 3; d4 = g % 4
                sm = lambda off, n: small(g % 6, off, n)
                zg = sm(4, 8)
                bP = bP1
                U_, VN_ = U[d2], VN[d2]
                bt = bP.next()
                for h in range(4):
                    S.tr(PBB[bt][:, h * 128:(h + 1) * 128], QTOK[d2][:, h * 128:(h + 1) * 128], IDB[:])
                for h in range(4):
                    S.tr(PBB[bt][:, 512 + h * 128:512 + (h + 1) * 128], KTOK[g % 4][:, h, :], IDB[:])
                cp("act", QT[g % 3], PBB[bt][:, 0:512].rearrange("p (h t) -> p h t", h=4))
                cp("dve", KT[g % 3], PBB[bt][:, 512:1024].rearrange("p (h t) -> p h t", h=4))
                yield
                st6 = sm(12, 6); mv = sm(18, 2); lrs = sm(20, 1); ltmp = sm(21, 1)
                bnstats(st6, VN_)
                bnaggr(mv, st6)
                rstd_from_ss(ltmp, mv[:, 1:2], 1, 1.0, lrs)
                ts("dve", VN_, VN_, mv[:, 0:1], ALU.subtract, lrs, ALU.mult)
                yield
                tt("pool", VN_, VN_, RV[:, RV_LNG:RV_LNG + 512], ALU.mult)
                tt("pool", VN_, VN_, RV[:, RV_LNB:RV_LNB + 512], ALU.add)
                cp("pool", VNB, VN_)
                if ty == 1:
                    S.dma(o_Vs, VN_)
                elif t["i"] == NPT - 1:
                    S.dma(o_Vp, VN_)
                yield
                bs_ = bP.next()
                ps_s = PB[bs_][:, :]
                for h in range(4):
                    S.mm(ps_s[:, h * 128:(h + 1) * 128], [(WM[:, ty * 4 + h, :], VNB[:, h * 128:(h + 1) * 128])])
                for h in range(4):
                    stt("dve", YGM[:, h * 128:(h + 1) * 128], ps_s[:, h * 128:(h + 1) * 128],
                        GN[:, GN_BS + ty * 4 + h:GN_BS + ty * 4 + h + 1], U_[:, h * 128:(h + 1) * 128], ALU.add, ALU.mult)
                yield
                by = bP.next()
                for h in range(4):
                    S.tr(PBB[by][:, h * 128:(h + 1) * 128], YGM[:, h * 128:(h + 1) * 128], IDB[:])
                cp("act", YT[g % 4][:, 0:4, :], PBB[by][:, 0:512].rearrange("p (k t) -> p k t", k=4))
                yield

            def stage_G(t):
                ty = t["ty"]; g = t["g"]
                sm = lambda off, n: small(g % 6, off, n)
                zg = sm(4, 8)
                bP = bG
                ig = sm(24, 4); e1 = sm(28, 4); l1 = sm(32, 4); bb = sm(36, 4); aa = sm(44, 4)
                mx = sm(48, 4); xf = sm(120, 4)
                tt("dve", ig, zg[:, 0:4], RV[:, RV_BI:RV_BI + 4], ALU.add)
                tt("dve", xf, zg[:, 4:8], RV[:, RV_BF:RV_BF + 4], ALU.add)
                act(e1, xf, AF.Exp, scale=-1.0)
                act(l1, e1, AF.Ln, bias=1.0)
                bc = bP.next()
                ps_c = PB[bc][:, 0:4]
                S.mm(ps_c, [(CM[:, CM_CUM + ty, :], l1)])
                ts("dve", bb, ps_c, -1.0, ALU.mult)
                tt("dve", aa, ig, bb, ALU.subtract)
                yield
                ba = bP.next()
                ps_A = PB[ba][:, :]
                bcast_rows(g % 6, 0, aa, DGA, ps_A)
                tt("dve", PBIG, ps_A.rearrange("p (h s) -> p h s", h=4),
                   CM[:, CM_MASK + ty, :].unsqueeze(1).to_broadcast([128, 4, 128]), ALU.add)
                redmax(mx, PBIG)
                tt("dve", mx, mx, bb, ALU.add)
                if ty == 1:
                    S.dma(NCOL[:], snT.rearrange("h d j -> d h j"))
                    for h in range(2):
                        S.dma(CSB3[h][:, :, 0:128], sC[:, h, :, :].rearrange("j d e -> d j e"), eng="pool")
                yield

            def stage_M1(t):
                ty = t["ty"]; g = t["g"]; d2 = g % 2; d3 = g % 3
                sm = lambda off, n: small(g % 6, off, n)
                bb = sm(36, 4); gg = sm(40, 4); aa = sm(44, 4); mx = sm(48, 4); mt = sm(52, 4); cc = sm(56, 4)
                wint = sm(60, 4); negm = sm(64, 4); emt = sm(68, 4); mb8 = sm(72, 8); lsel = sm(80, 8)
                wend = sm(88, 4); dec = sm(92, 4); tmp4 = sm(124, 4); tmp5 = sm(128, 4)
                mprev = MRUN[:] if ty == 0 else SMT[:]
                tt("dve", gg, bb, mprev, ALU.add)
                tt("dve", mt, mx, gg, ALU.max)
                tt("dve", cc, bb, mt, ALU.subtract)
                cp("dve", mb8[:, 0:4], mt)
                cp("dve", mb8[:, 4:8], bb)
                bl = bM1.next()
                ps_l = PB[bl][:, 0:8]
                S.mm(ps_l, [(CM[:, CM_SEL + ty, :], mb8)])
                cp("dve", lsel, ps_l)
                if ty == 0:
                    yield
                ts("dve", negm, mt, -1.0, ALU.mult)
                tt("dve", tmp4, gg, mt, ALU.subtract)
                act(wint, tmp4, AF.Exp)
                act(emt, negm, AF.Exp)
                yield
                bC = bM1.next()
                ps_C = PB[bC][:, :]
                bcast_rows(g % 6, 8, cc, DGC, ps_C)
                tt("dve", BIG1, ps_C.rearrange("p (h s) -> p h s", h=4),
                   CM[:, CM_MASKT + ty, :].unsqueeze(1).to_broadcast([128, 4, 128]), ALU.add)
                yield
                for h in range(4):
                    act(BIG1[:, h, :], BIG1[:, h, :], AF.Exp, bias=aa[:, h:h + 1])
                bsc = bM1.next()
                ps_sc = PB[bsc][:, :].rearrange("p (h t) -> p h t", h=4)
                for h in range(4):
                    S.mm(ps_sc[:, h, :], [(KT[g % 3][:, h, :], QT[g % 3][:, h, :])])
                tt("dve", STB, ps_sc, BIG1, ALU.mult)
                yield
                bx, by = bM1.next(), bM1.next()
                for h in range(4):
                    S.mm(hv(bx, by, h), [(STB[:, h, :], VAUG[g % 4][:, h, :])])
                tt("dve", tmp4, aa, lsel[:, 4:8], ALU.add)
                tt("dve", tmp4, tmp4, lsel[:, 0:4], ALU.subtract)
                act(wend, tmp4, AF.Exp)
                tt("dve", tmp5, lsel[:, 4:8], mprev, ALU.add)
                tt("dve", tmp5, tmp5, lsel[:, 0:4], ALU.subtract)
                act(dec, tmp5, AF.Exp)
                vw_t = VW if ty == 0 else VWS[:, :, 0:129]
                tt("pool", vw_t, VAUG[g % 4], wend.unsqueeze(2).to_broadcast([128, 4, 129]), ALU.mult)
                if ty == 1:
                    cp("pool", KTOKS[:], KTOK[g % 4])
                yield
                num = NUM[d2]
                if ty == 0:
                    cx, cy = bM1.next(), bM1.next()
                    for half, bk in enumerate((bx, by)):
                        cp("dve", num.rearrange("p (a b) e -> p a b e", b=2)[:, :, half, :],
                           PB[bk][:, 0:258].rearrange("p (h e) -> p h e", h=2))
                    for h in range(4):
                        S.mm(hv(cx, cy, h), [(QT[g % 3][:, h, :], CSTB[:, h, :])])
                    for h in range(4):
                        act(QCS[:, h, :], hv(cx, cy, h), AF.Copy, scale=wint[:, h:h + 1])
                    tt("dve", num, num, QCS, ALU.add)
                    yield
                    ux, uy = bM1.next(), bM1.next()
                    for h in range(4):
                        S.mm(hv(ux, uy, h), [(KTOK[g % 4][:, h, :], VW[:, h, :])])
                    for h in range(4):
                        stt("dve", CST[:, h, :], CST[:, h, :], dec[:, h:h + 1], hv(ux, uy, h), ALU.mult, ALU.add)
                    cp("pool", CSTB[:], CST[:])
                    cp("dve", MRUN[:], lsel[:, 0:4])
                    if t["i"] == NPT - 1:
                        for h in range(4):
                            S.dma(o_Cp[h], CST[:, h, 0:128])
                        S.dma(o_Np.rearrange("h d -> d h"), CST[:, :, 128], allow_slow_non_contiguous=True)
                        S.dma(o_Mp, lsel[0:1, 0:4])
                    yield
                else:
                    for half, bk in enumerate((bx, by)):
                        cp("dve", num.rearrange("p (a b) e -> p a b e", b=2)[:, :, half, :],
                           PB[bk][:, 0:258].rearrange("p (h e) -> p h e", h=2))
                    tt("dve", DECS, dec.unsqueeze(1).to_broadcast([128, 16, 4]),
                       CS[:, 0:16].unsqueeze(2).to_broadcast([128, 16, 4]), ALU.mult)
                    bd = bM1.next()
                    ps_d = PB[bd][:, 0:64]
                    S.mm(ps_d, [(ONESF, DECS.rearrange("p j h -> p (j h)"))])
                    cp("dve", DECB[:].rearrange("p j h -> p (j h)"), ps_d)
                    pst = list(mt.ap[0])[0]
                    S.dma(o_Ms, bass.AP(mt.tensor, mt.offset + 7 * pst, [[8 * pst, 16], [1, 4]]))
                    yield
                    for h in range(4):
                        CSB = CSB3[h % 2]
                        if h >= 2:
                            S.dma(CSB[:, :, 0:128], sC[:, h, :, :].rearrange("j d e -> d j e"), eng="pool")
                        cp("dve", CSB[:, :, 128], NCOL[:, h, :])
                        dst = bass.AP(QTD.tensor, QTD.offset, [list(QTD.ap[0]), [128 + 8, 16], [1, 8]])
                        cp("dve", dst, QT[g % 3][:, h, :].rearrange("p (j t) -> p j t", j=16))
                        bq = bM1.next()
                        pq = PB[bq][:, 0:129]
                        S.mm(pq, [(QTD[:, j, :], CSB[:, j, :]) for j in range(16)])
                        act(QCS[:, h, :], pq, AF.Copy, scale=wint[:, h:h + 1])
                        tt("dve", num[:, h, :], num[:, h, :], QCS[:, h, :], ALU.add)
                        yield

            def stage_M2(t):
                g = t["g"]; d2 = g % 2; d3 = g % 3
                sm = lambda off, n: small(g % 6, off, n)
                emt = sm(68, 4); aden = sm(96, 4); rden = sm(100, 4)
                ssq = sm(104, 4); r2 = sm(108, 4); rs2 = sm(112, 4); scl = sm(116, 4); rtmp = sm(132, 4)
                num = NUM[d2]
                den = num[:, :, 128]
                stt("dve", aden, den, -1.0, den, ALU.mult, ALU.max)
                tt("dve", aden, aden, emt, ALU.max)
                recip(rden, aden)
                memset("pool", ssq, 0.0)
                for h in range(4):
                    act(JUNK, num[:, h, 0:128], AF.Square, accum=ssq[:, h:h + 1])
                yield
                tt("dve", r2, rden, rden, ALU.mult)
                tt("dve", r2, r2, ssq, ALU.mult)
                rstd_from_ss(rtmp, r2, 4, 1.0 / 128, rs2)
                tt("dve", scl, rden, rs2, ALU.mult)
                yield
                for h in range(4):
                    stt("dve", YML[:, h * 128:(h + 1) * 128], num[:, h, 0:128], scl[:, h:h + 1], SGM[g % 5][:, h, :],
                        ALU.mult, ALU.mult)
                yield
                b7 = bM2.next()
                for h in range(4):
                    S.tr(PBB[b7][:, h * 128:(h + 1) * 128], YML[:, h * 128:(h + 1) * 128], IDB[:])
                cp("act", YT[g % 4][:, 4:8, :], PBB[b7][:, 0:512].rearrange("p (k t) -> p k t", k=4))
                yield
                for n in range(2):
                    bh = bM2.next()
                    ps_h = PB[bh][:, :]
                    S.mm(ps_h, [(YT[g % 4][:, c, :], WOUT[:, c, n * 512:(n + 1) * 512]) for c in range(8)])
                    hs = H[:, t["slot"], n * 512:(n + 1) * 512]
                    tt("dve", hs, hs, ps_h, ALU.add)
                    yield

            S.marks.append(("A start", sg_i, max(S.free.values())))
            S.capture = []
            load_q(0)
            bgQ0 = S.capture
            S.capture = None
            run_pipeline(sg, [stage_N, stage_P, stage_P1, stage_G, stage_M1, stage_M2],
                         background={0: bgA, len(sg) + 1: bgQ0})
            S.marks.append(("A end", sg_i, max(S.free.values())))

            B = Arena(R2_OFF)
            nt = len(sg)
            NTOK = nt * 128
            A2T = B.bf(8 * 9 * 128, ("p (k n) -> p k n", dict(k=8)))
            ABF = B.bf(DM)
            GT = [B.bf(8 * 512, ("p (k n) -> p k n", dict(k=8))) for _ in range(2)]
            wd1_chk = B.bf(8 * 1024)
            assert wd1_chk.offset == WD[1].offset, (wd1_chk.offset, WD[1].offset)
            RL = [B.f32(512) for _ in range(2)]
            assert B.off <= WA_OFF, ("phase B low region overflows", B.off, WA_OFF)
            B = Arena(WA_OFF + 8 * INC * 2)
            if has_s:
                CSF2 = [B.f32(8 * 129, ("p (j e) -> p j e", dict(j=8))) for _ in range(3)]
                KMK2 = [B.bf(8 * 128, ("p (j d) -> p j d", dict(j=8)))]
                NOUT = B.f32(64, ("p (h j) -> p h j", dict(h=4)))

            def sample_state_update():
                bU = Banks([6, 7])

                def load_chunk(c):
                    h_, j0 = c // 2, (c % 2) * 8
                    S.dma(CSF2[c % 3][:, :, 0:128], sC[j0:j0 + 8, h_, :, :].rearrange("j d e -> d j e"))
                load_chunk(0)
                load_chunk(1)
                load_chunk(2)
                for h in range(4):
                    for half in range(2):
                        c = h * 2 + half
                        j0 = half * 8
                        csf = CSF2[c % 3]
                        kmk = KMK2[0]
                        cp("dve", csf[:, :, 128], NCOL[:, h, j0:j0 + 8])
                        tt("pool", kmk, KTOKS[:, h, :].unsqueeze(1).to_broadcast([128, 8, 128]),
                           CS[:, 16 + j0:24 + j0].unsqueeze(2).to_broadcast([128, 8, 128]), ALU.mult)
                        for grp3 in ((0, 1, 2), (3, 4, 5), (6, 7)):
                            bj = bU.next()
                            for k3, jj in enumerate(grp3):
                                S.mm(PB[bj][:, k3 * 129:(k3 + 1) * 129], [(kmk[:, jj, :], VWS[:, h, 0:129])])
                            for k3, jj in enumerate(grp3):
                                stt("dve", csf[:, jj, :], csf[:, jj, :], DECB[:, j0 + jj, h:h + 1],
                                    PB[bj][:, k3 * 129:(k3 + 1) * 129], ALU.mult, ALU.add)
                        S.dma(o_Cs[j0:j0 + 8, h, :, :].rearrange("j d e -> d j e"), csf[:, :, 0:128])
                        cp("dve", NOUT[:, h, j0:j0 + 8], csf[:, :, 128])
                        if c + 3 < 8:
                            load_chunk(c + 3)
                S.dma(o_NsT.rearrange("h d j -> d h j"), NOUT)

            S.capture = []

            def norm_b(t):
                norm_T(t["slot"] % 5, H[:, t["slot"], :], ABF,
                       A2T[:, :, t["slot"] * 128:(t["slot"] + 1) * 128], 136, GN_FFN, 6 + t["slot"] % 2)

            S.capture = None
            GSZ = 384 if NTOK % 384 == 0 else 512
            for t in sg[0:GSZ // 128]:
                norm_b(t)
            late_norm = list(sg[GSZ // 128:])
            S.capture = []
            load_cast(WG, w_pg.rearrange("(k p) n -> p k n", p=128))
            load_cast(WP, w_pp.rearrange("(k p) n -> p k n", p=128))
            S.dma(NF, nfin.partition_broadcast(128))
            tgroups = [(g0, min(g0 + GSZ, NTOK)) for g0 in range(0, NTOK, GSZ)]
            gi_box = [0]

            def emit_up(qd, g0, g1):
                wu = WU[qd % 2]
                n = g1 - g0
                gt = GT[gi_box[0] % 2]
                gi_box[0] += 1
                for fc in range(8):
                    ps = PB[2 + fc % 4][:, 0:n]
                    S.mm(ps, [(wu[:, kc, fc * 128:(fc + 1) * 128], A2T[:, kc, g0:g1]) for kc in range(8)])
                    rl = RL[fc % 2]
                    act(rl[:, 0:n], ps, AF.Relu)
                    tt("dve", gt[:, fc, 0:n], rl[:, 0:n], rl[:, 0:n], ALU.mult)
                return gt

            def emit_down(qd, g0, g1, gt):
                wd = WD[qd % 2]
                for ti in range(g0 // 128, g1 // 128):
                    lt = slice(ti * 128 - g0, ti * 128 - g0 + 128)
                    for nn in range(2):
                        ps_h = PB[nn][:, :]
                        S.mm(ps_h, [(gt[:, fc, lt], wd[:, fc, nn * 512:(nn + 1) * 512]) for fc in range(8)])
                        hs = H[:, ti, nn * 512:(nn + 1) * 512]
                        tt("dve", hs, hs, ps_h, ALU.add)

            def emit_group(qd, g0, g1):
                wu, wd = WU[qd % 2], WD[qd % 2]
                n = g1 - g0
                gt = GT[gi_box[0] % 2]
                gi_box[0] += 1
                for fc in range(8):
                    ps = PB[2 + fc % 4][:, 0:n]
                    S.mm(ps, [(wu[:, kc, fc * 128:(fc + 1) * 128], A2T[:, kc, g0:g1]) for kc in range(8)])
                    rl = RL[fc % 2]
                    act(rl[:, 0:n], ps, AF.Relu)
                    tt("dve", gt[:, fc, 0:n], rl[:, 0:n], rl[:, 0:n], ALU.mult)
                for ti in range(g0 // 128, g1 // 128):
                    lt = slice(ti * 128 - g0, ti * 128 - g0 + 128)
                    for nn in range(2):
                        ps_h = PB[nn][:, :]
                        S.mm(ps_h, [(gt[:, fc, lt], wd[:, fc, nn * 512:(nn + 1) * 512]) for fc in range(8)])
                        hs = H[:, ti, nn * 512:(nn + 1) * 512]
                        tt("dve", hs, hs, ps_h, ALU.add)

            load_q(1)
            emit_group(0, *tgroups[0])
            x_stream = S.capture
            S.capture = []
            for t in late_norm:
                norm_b(t)
            y_stream = S.capture
            S.capture = None
            S.run_streams([x_stream, y_stream])

            S.capture = []
            seq = [(qd, gi_, g0, g1) for qd in range(4) for gi_, (g0, g1) in enumerate(tgroups)
                   if not (qd == 0 and gi_ == 0)]
            pend_ = None
            for k_ in range(len(seq) + 1):
                if k_ < len(seq):
                    qd, gi_, g0, g1 = seq[k_]
                    gt_new = emit_up(qd, g0, g1)
                if pend_ is not None:
                    pq, pgi, pg0, pg1, pgt = pend_
                    emit_down(pq, pg0, pg1, pgt)
                    if pgi == len(tgroups) - 1 and pq + 2 < 4:
                        load_q(pq + 2)
                pend_ = (qd, gi_, g0, g1, gt_new) if k_ < len(seq) else None
            b_stream = S.capture
            S.capture = None
            streams_b = [b_stream]
            if has_s:
                S.capture = []
                sample_state_update()
                streams_b.append(S.capture)
                S.capture = None
            S.run_streams(streams_b)

            bgC = load_phaseA_weights(False) if sg_i + 1 < len(SGS) else None
            C = Arena(R2_OFF)
            A3 = [C.bf(DM) for _ in range(2)]
            A3T = [C.bf(8 * 128, ("p (k n) -> p k n", dict(k=8))) for _ in range(2)]
            PF = [C.f32(PLE) for _ in range(2)]
            PBF = [C.bf(PLE) for _ in range(2)]
            PT = [C.bf(2 * 128, ("p (k n) -> p k n", dict(k=2))) for _ in range(2)]
            THC = [C.f32(DM) for _ in range(2)]
            TC = [C.f32(DM) for _ in range(3)]
            YO = [C.f32(DM) for _ in range(2)]
            JK = C.bf(DM)
            assert C.off <= WA_OFF, ("phase C overlaps prefetched weights", C.off, WA_OFF)
            bC0 = Banks([0, 1])
            bC1 = Banks([2, 3, 4, 5])

            def stage_C0(t):
                par = t["slot"] % 2
                d5 = t["slot"] % 5
                hsl = H[:, t["slot"], :]
                S.dma(PF[par], t["p"])
                ss = small(d5, 140, 1); rs = small(d5, 141, 1); tmp = small(d5, 142, 1)
                memset("pool", ss, 0.0)
                act(A3[par], hsl, AF.Square, accum=ss)
                rstd_from_ss(tmp, ss, 1, 1.0 / DM, rs)
                act(A3[par], hsl, AF.Copy, scale=rs)
                cp("pool", PBF[par], PF[par])
                yield

            def stage_C0b(t):
                par = t["slot"] % 2
                bn_ = bC0.next()
                for kc in range(8):
                    S.tr(PBB[bn_][:, kc * 128:(kc + 1) * 128], A3[par][:, kc * 128:(kc + 1) * 128], IDB[:])
                tt("dve", A3T[par], PBB[bn_][:, 0:1024].rearrange("p (k t) -> p k t", k=8),
                   GN[:, GN_PLE:GN_PLE + 8].unsqueeze(2).to_broadcast([128, 8, 128]), ALU.mult)
                bt = bC0.next()
                for c in range(2):
                    S.tr(PBB[bt][:, c * 128:(c + 1) * 128], PBF[par][:, c * 128:(c + 1) * 128], IDB[:])
                cp("act", PT[par], PBB[bt][:, 0:256].rearrange("p (k t) -> p k t", k=2))
                yield

            def stage_C1(t):
                par = t["slot"] % 2
                p3 = t["slot"] % 3
                for nn in range(2):
                    ps_g = PB[bC1.next()][:, :]
                    S.mm(ps_g, [(A3T[par][:, kc, :], WG[:, kc, nn * 512:(nn + 1) * 512]) for kc in range(8)])
                    act(THC[par][:, nn * 512:(nn + 1) * 512], ps_g, AF.Tanh, scale=0.5)
                    ps_p = PB[bC1.next()][:, :]
                    S.mm(ps_p, [(PT[par][:, c, :], WP[:, c, nn * 512:(nn + 1) * 512]) for c in range(2)])
                    stt("dve", TC[p3][:, nn * 512:(nn + 1) * 512], THC[par][:, nn * 512:(nn + 1) * 512], 1.0, ps_p,
                        ALU.add, ALU.mult)
                    yield

            def stage_C2(t):
                p3 = t["slot"] % 3
                d5 = t["slot"] % 5
                hsl = H[:, t["slot"], :]
                stt("dve", hsl, TC[p3], 0.5, hsl, ALU.mult, ALU.add)
                ss = small(d5, 144, 1)
                memset("pool", ss, 0.0)
                act(JK, hsl, AF.Square, accum=ss)
                yield

            def stage_C3(t):
                par = t["slot"] % 2
                d5 = t["slot"] % 5
                hsl = H[:, t["slot"], :]
                ss = small(d5, 144, 1); rs = small(d5, 145, 1); tmp = small(d5, 146, 1)
                rstd_from_ss(tmp, ss, 1, 1.0 / DM, rs)
                stt("dve", YO[par], hsl, rs, NF, ALU.mult, ALU.mult)
                S.dma(t["y"], YO[par])
                yield

            S.marks.append(("B end", sg_i, max(S.free.values())))
            run_pipeline(sg, [stage_C0, stage_C0b, stage_C1, stage_C2, stage_C3], background={0: bgC})
            S.marks.append(("C end", sg_i, max(S.free.values())))

        S.finish()
    return nc, S


_CACHE = {}


def _consts():
    idx = np.arange(128)
    seq = idx // 8
    ident = np.eye(128, dtype=np.float32)
    ones = np.ones((128, 128), np.float32)
    causal_ts = (idx[None, :] <= idx[:, None])
    same = (seq[:, None] == seq[None, :])
    cum_P = causal_ts.T.astype(np.float32)
    cum_S = (causal_ts.T & same).astype(np.float32)
    mask_P = np.where(causal_ts, 0.0, NEG).astype(np.float32)
    mask_S = np.where(causal_ts & same, 0.0, NEG).astype(np.float32)
    maskT_P = mask_P.T.copy()
    maskT_S = mask_S.T.copy()
    sel_P = np.zeros((128, 128), np.float32); sel_P[127, :] = 1.0
    sel_S = (idx[:, None] == (seq[None, :] * 8 + 7)).astype(np.float32)
    t01_P = causal_ts.astype(np.float32)
    bd01_S = (causal_ts & same).astype(np.float32)
    cmat = np.stack([ident, ones, cum_P, cum_S, mask_P, mask_S, maskT_P, maskT_S, sel_P, sel_S, t01_P, bd01_S])
    lastmask = (idx[:, None] == (np.arange(16)[None, :] * 8 + 7)).astype(np.float32)
    seqmask = (seq[:, None] == np.arange(16)[None, :]).astype(np.float32)
    csm = np.concatenate([lastmask, seqmask], axis=1)
    return np.ascontiguousarray(cmat), np.ascontiguousarray(csm)


def kernel(x_prompt, x_sample, p_prompt, p_sample, state_C, state_n, state_m,
           norm_mix, w_in, gm_ln_g, gm_ln_b, gm_ws, gm_bs, ml_b_i, ml_b_f, ml_norm,
           w_out, norm_ffn, w_up, w_down, norm_ple, w_ple_gate, w_ple_proj, norm_final):
    f = lambda a: np.ascontiguousarray(np.asarray(a, dtype=np.float32))
    x_prompt, x_sample, p_prompt, p_sample = f(x_prompt), f(x_sample), f(p_prompt), f(p_sample)
    state_C, state_n, state_m = f(state_C), f(state_n), f(state_m)
    if "nc" not in _CACHE:
        _CACHE["nc"] = build()[0]
    nc = _CACHE["nc"]
    cmat, csm = _consts()
    gbs = f(gm_bs)[0]
    gains = np.concatenate([f(norm_mix)[0].reshape(8, 128).T, f(norm_ffn)[0].reshape(8, 128).T,
                            f(norm_ple)[0].reshape(8, 128).T, gbs.T, np.tile(gbs[:, :8], (1, 16)).T], axis=1)
    rowv = np.concatenate([f(gm_ln_g)[0], f(gm_ln_b)[0], f(ml_b_i)[0], f(ml_b_f)[0]])
    gws = f(gm_ws)[0]
    wsg = np.concatenate([gws, np.tile(gws[:, :8, :8], (1, 16, 16))], axis=0)
    shared = dict(w_in=f(w_in)[0], w_out=f(w_out)[0], w_up=f(w_up)[0], w_down=f(w_down)[0],
                  w_pg=f(w_ple_gate)[0], w_pp=f(w_ple_proj)[0], gains=f(gains), rowv=f(rowv),
                  nfin=f(norm_final), mlg=f(ml_norm)[0], wsg=f(wsg), cmat=cmat, csm=csm)
    in_maps = []
    for c in range(NCORES):
        sl = slice(16 * c, 16 * (c + 1))
        m = dict(shared)
        m.update(xp=x_prompt[c], xs=x_sample[sl].reshape(128, DM), pp=p_prompt[0, c],
                 psm=p_sample[0, sl].reshape(128, PLE), sC=state_C[0, sl],
                 snT=np.ascontiguousarray(state_n[0, sl].transpose(1, 2, 0)),
                 smt=np.ascontiguousarray(np.repeat(state_m[0, sl], 8, axis=0)))
        in_maps.append(m)
    res = run_bass_kernel_spmd(nc, in_maps, core_ids=list(range(NCORES)))
    R = res.results
    g = lambda k: [np.asarray(R[c][k], dtype=np.float32) for c in range(NCORES)]
    y_prompt = np.stack(g("y_p"))
    y_sample = np.concatenate(g("y_s")).reshape(128, 8, DM)
    Cp = np.stack(g("o_Cp"))[None]
    Np_ = np.stack(g("o_Np"))[None]
    Mp = np.stack([a.reshape(4) for a in g("o_Mp")])[None]
    Vp = np.stack(g("o_Vp"))[None]
    Cs = np.concatenate(g("o_Cs"))[None]
    Ns = np.concatenate([a.transpose(2, 0, 1) for a in g("o_NsT")])[None]
    Ms = np.concatenate(g("o_Ms"))[None]
    Vs = np.concatenate(g("o_Vs")).reshape(128, 8, 512)[None]
    return (y_prompt, y_sample, Cp, Np_, Mp, Vp, Cs, Ns, Ms, Vs)
```

```python
import numpy as np
import concourse.bass as bass
import concourse.mybir as mybir
from concourse.bass_utils import run_bass_kernel_spmd

F32 = mybir.dt.float32
BF16 = mybir.dt.bfloat16
AF = mybir.ActivationFunctionType
ALU = mybir.AluOpType
AX = mybir.AxisListType

EPOCH = 30000
NDMASEM = 12
NEG = -30000.0
EPS = 1e-6
NCORES = 8


def _dsize(dt):
    return mybir.dt.size(dt)


class Tok:
    __slots__ = ("key", "val", "clock", "fin")

    def __init__(self, key, val, clock):
        self.key, self.val, self.clock = key, val, clock
        self.fin = 0.0


def _nelem(ap):
    n = 1
    for (_s, c) in list(ap.ap)[1:]:
        n *= c
    return n


SYNC_LAT = 0.35
PE_WAIT_MARGIN = 2.0


class Sched:
    ENG = ("pe", "act", "dve", "pool", "sp")

    def __init__(self, nc):
        self.nc = nc
        self.prog = {e: [] for e in self.ENG}
        self.count = {e: 0 for e in self.ENG}
        self.knows = {e: {} for e in self.ENG}
        self.sems = {}
        self.dma_i = {e: 0 for e in self.ENG}
        self.dma_cnt = {}
        self.dma_last = {}
        self.bufs = {}
        self._ctx = []
        self.nwaits = 0
        self.nops = 0
        self.marks = []
        self.capture = None
        self.free = {e: 0.0 for e in self.ENG}

    def _cost(self, eng, writes, reads):
        ap = writes[0] if writes else (reads[0] if reads else None)
        n = _nelem(ap) if ap is not None else 1
        if reads:
            n = max(n, max(_nelem(r) for r in reads) if eng != "pe" else n)
        if eng == "dve":
            return 0.06 + n * 0.00125
        if eng == "act":
            return 0.2 + n * 0.0008
        if eng == "pool":
            return 0.1 + n * 0.002
        return 0.1

    def _est_start(self, eng, deps):
        t = self.free[eng]
        for d in deps:
            f = d.fin + (0.0 if d.key == ("e", eng) else SYNC_LAT)
            if f > t:
                t = f
        return t

    def mark(self, kind, bank):
        if self.capture is not None:
            self.capture.append(("bank", kind, bank, None, None, 0.0))

    def run_streams(self, streams):
        ptr = [0] * len(streams)
        holder = {}
        while True:
            best = None
            for i, stm in enumerate(streams):
                blocked = False
                while ptr[i] < len(stm) and stm[ptr[i]][0] == "bank":
                    _, mk, bank_, _, _, _ = stm[ptr[i]]
                    if mk == "begin":
                        if holder.get(bank_) not in (None, i):
                            blocked = True
                            break
                        holder[bank_] = i
                    else:
                        holder[bank_] = None
                    ptr[i] += 1
                if blocked or ptr[i] >= len(stm):
                    continue
                kind, eng, a, b, c, cost = stm[ptr[i]]
                if kind == "op":
                    deps, _, _ = self._deps_for(b, c, eng)
                else:
                    deps, _, _ = self._deps_for([b], [a], "dma")
                st_ = self._est_start(eng, deps)
                if eng == "pe" and st_ > self.free[eng] + 1e-9:
                    st_ += PE_WAIT_MARGIN
                if best is None or (st_, i) < best[0]:
                    best = ((st_, i), i)
            if best is None:
                break
            i = best[1]
            kind, eng, a, b, c, cost = streams[i][ptr[i]]
            ptr[i] += 1
            if kind == "op":
                self.op(eng, a, b, c, cost=cost)
            else:
                self.dma(a, b, eng=eng, **c)

    def _sem(self, key):
        s = self.sems.get(key)
        if s is None:
            cm = self.nc.semaphore("s_" + "_".join(str(k) for k in key))
            s = cm.__enter__()
            self._ctx.append(cm)
            self.sems[key] = s
        return s

    def _eng_sem_val(self, eng, v):
        ep = (v - 1) // EPOCH
        return self._sem((eng, ep)), v - ep * EPOCH

    @staticmethod
    def _intervals(pairs, limit=64):
        pairs = [(abs(st), c) for (st, c) in pairs if c > 1]
        if not pairs:
            return [(0, 1)]
        pairs.sort(key=lambda p: -p[0])
        out = [(0, 0)]
        res = []

        def rec(base, idx, budget):
            rest = pairs[idx:]
            ext = sum((c - 1) * st for st, c in rest) + 1
            if not rest:
                return [(base, base + 1)]
            st0, c0 = rest[0]
            inner = sum((c - 1) * st for st, c in rest[1:]) + 1
            if st0 > inner and c0 <= budget:
                r = []
                for i in range(c0):
                    r += rec(base + i * st0, idx + 1, max(1, budget // c0))
                return r
            return [(base, base + ext)]
        return rec(0, 0, limit)

    @staticmethod
    def regions(ap):
        t = ap.tensor
        name = t.name
        esz = _dsize(ap.dtype)
        pairs = [tuple(p) for p in ap.ap]
        tn = type(t).__name__
        if not ("SB" in tn or "PSum" in tn):
            return [(name, 0, 1, (ap.offset + lo) * esz, (ap.offset + hi) * esz)
                    for (lo, hi) in Sched._intervals(pairs)]
        prow = 1
        for d in list(t.shape)[1:]:
            prow *= d
        prow_b = prow * _dsize(t.dtype)
        if "PSum" in tn:
            return [("~" + name, 0, 128, 0, prow_b)]
        off_b = ap.offset * esz
        p0 = off_b // prow_b
        lo0 = off_b - p0 * prow_b
        pstep, pcnt = pairs[0]
        if pcnt > 1:
            pst = (pstep * esz) // prow_b
            p1 = p0 + (pcnt - 1) * max(pst, 1) + 1
        else:
            p1 = p0 + 1
        return [(name, p0, p1, (lo0 + lo * esz) // 4 * 4, (lo0 + hi * esz + 3) // 4 * 4)
                for (lo, hi) in Sched._intervals(pairs[1:])]

    def _deps_for(self, reads, writes, eng=None):
        deps = []
        rr = [r for a in reads for r in self.regions(a)]
        wr = [r for a in writes for r in self.regions(a)]
        for (name, p0, p1, lo, hi) in rr:
            b = self.bufs.get(name)
            if b is None:
                continue
            if name[0] == "~":
                for (q0, q1, l2, h2, tok) in b["w"]:
                    e2, w2 = b["meta"]
                    if e2 == eng and not w2:
                        continue
                    deps.append(tok)
                continue
            for (q0, q1, l2, h2, tok) in b["w"]:
                if lo < h2 and l2 < hi and p0 < q1 and q0 < p1:
                    deps.append(tok)
        for (name, p0, p1, lo, hi) in wr:
            b = self.bufs.get(name)
            if b is None:
                continue
            if name[0] == "~":
                for (q0, q1, l2, h2, tok) in b["w"]:
                    e2, w2 = b["meta"]
                    if e2 == eng and eng == "pe":
                        continue
                    deps.append(tok)
                continue
            for (q0, q1, l2, h2, tok) in b["w"]:
                if lo < h2 and l2 < hi and p0 < q1 and q0 < p1:
                    deps.append(tok)
            for (q0, q1, l2, h2, tok) in b["r"]:
                if lo < h2 and l2 < hi and p0 < q1 and q0 < p1:
                    deps.append(tok)
        return deps, rr, wr

    def _record(self, rr, wr, tok, eng=None):
        for (name, p0, p1, lo, hi) in wr:
            b = self.bufs.setdefault(name, {"w": [], "r": []})
            if name[0] == "~":
                b["w"] = [(p0, p1, lo, hi, tok)]
                b["meta"] = (eng, True)
                continue
            b["w"] = [e for e in b["w"] if not (p0 <= e[0] and e[1] <= p1 and lo <= e[2] and e[3] <= hi)]
            b["r"] = [e for e in b["r"] if not (p0 <= e[0] and e[1] <= p1 and lo <= e[2] and e[3] <= hi)]
            b["w"].append((p0, p1, lo, hi, tok))
        for (name, p0, p1, lo, hi) in rr:
            b = self.bufs.setdefault(name, {"w": [], "r": []})
            if name[0] == "~":
                b["w"] = [(p0, p1, lo, hi, tok)]
                b["meta"] = (eng, False)
                continue
            b["r"] = [e for e in b["r"] if not (e[0] == p0 and e[1] == p1 and e[2] == lo and e[3] == hi
                                                and e[4].key == tok.key and e[4].val <= tok.val)]
            b["r"].append((p0, p1, lo, hi, tok))

    def _waits(self, eng, deps):
        kn = self.knows[eng]
        need = {}
        for t in deps:
            if kn.get(t.key, 0) >= t.val:
                continue
            cur = need.get(t.key)
            if cur is None or cur.val < t.val:
                need[t.key] = t
        out = []
        for t in sorted(need.values(), key=lambda t: -len(t.clock)):
            if kn.get(t.key, 0) >= t.val:
                continue
            out.append(t)
            for k, v in t.clock.items():
                if kn.get(k, 0) < v:
                    kn[k] = v
        return out

    def _emit_waits(self, eng, toks):
        for t in toks:
            self.nwaits += 1
            if t.key[0] == "d":
                sem, val = self._sem(t.key), t.val
            else:
                sem, val = self._eng_sem_val(t.key[1], t.val)
            self.prog[eng].append(lambda e, sem=sem, val=val: e.wait_ge(sem, val))

    def op(self, eng, fn, reads, writes, cost=None):
        if cost is None:
            cost = self._cost(eng, writes, reads)
        if self.capture is not None:
            self.capture.append(("op", eng, fn, reads, writes, cost))
            return None
        self.nops += 1
        deps, rr, wr = self._deps_for(reads, writes, eng)
        start = self._est_start(eng, deps)
        w = self._waits(eng, deps)
        self._emit_waits(eng, w)
        self.count[eng] += 1
        v = self.count[eng]
        sem, _lv = self._eng_sem_val(eng, v)
        key = ("e", eng)
        clock = dict(self.knows[eng])
        clock[key] = v
        tok = Tok(key, v, clock)
        tok.fin = start + cost
        self.free[eng] = start + cost
        self.prog[eng].append(lambda e, fn=fn, sem=sem: fn(e).then_inc(sem, 1))
        self._record(rr, wr, tok, eng)
        return tok

    def dma(self, out, in_, eng="sp", **kw):
        if self.capture is not None:
            self.capture.append(("dma", eng, out, in_, kw, 0.0))
            return None
        self.nops += 1
        deps, rr, wr = self._deps_for([in_], [out], "dma")
        start = self._est_start(eng, deps)
        nbytes = _nelem(out) * 128 * _dsize(out.dtype)
        self.free[eng] = start + (0.6 if eng == "pool" else 0.08)
        i = self.dma_i[eng]
        self.dma_i[eng] += 1
        skey = ("d", eng, i % NDMASEM)
        prev = self.dma_last.get(skey)
        if prev is not None:
            deps.append(prev)
        w = self._waits(eng, deps)
        self._emit_waits(eng, w)
        val = self.dma_cnt.get(skey, 0) + 16
        self.dma_cnt[skey] = val
        sem = self._sem(skey)
        clock = dict(self.knows[eng])
        clock[skey] = val
        tok = Tok(skey, val, clock)
        tok.fin = start + 2.0 + nbytes / 80e3
        self.dma_last[skey] = tok
        self.prog[eng].append(
            lambda e, out=out, in_=in_, sem=sem, kw=kw: e.dma_start(out=out, in_=in_, **kw).then_inc(sem, 16))
        self._record(rr, wr, tok, "dma")
        return tok

    def mm(self, out, pairs):
        n = len(pairs)
        reads = []
        for (l, r) in pairs:
            reads += [l, r]

        def fn(e):
            ins = None
            for i, (l, r) in enumerate(pairs):
                ins = e.matmul(out, l, r, start=(i == 0), stop=(i == n - 1))
            return ins
        f32 = 8.0 if _dsize(pairs[0][1].dtype) == 4 else 1.0
        cost = sum((max(_nelem(r), 128) / 2100.0 + 0.01) * f32 for (_l, r) in pairs)
        return self.op("pe", fn, reads, [out], cost=cost)

    def tr(self, out, in_, ident):
        return self.op("pe", lambda e: e.transpose(out, in_, ident), [in_, ident], [out], cost=0.14)

    def finish(self):
        w = self._waits("sp", list(self.dma_last.values()))
        self._emit_waits("sp", w)
        nc = self.nc
        prog = self.prog
        with nc.Block() as block:
            @block.tensor
            def _(e):
                for f in prog["pe"]:
                    f(e)

            @block.scalar
            def _(e):
                for f in prog["act"]:
                    f(e)

            @block.vector
            def _(e):
                for f in prog["dve"]:
                    f(e)

            @block.gpsimd
            def _(e):
                for f in prog["pool"]:
                    f(e)

            @block.sync
            def _(e):
                for f in prog["sp"]:
                    f(e)
        for cm in reversed(self._ctx):
            cm.__exit__(None, None, None)
        self._ctx = []


DM = 1024
INC = 3080
DFF = 4096
PLE = 256
NPT = 16
ARENA_F32 = 37376
CM_IDENT, CM_ONES, CM_CUM, CM_MASK, CM_MASKT, CM_SEL, CM_01 = 0, 1, 2, 4, 6, 8, 10
RV_LNG, RV_LNB, RV_BI, RV_BF, RV_MLG = 0, 512, 1024, 1028, 1032
RV_N = RV_MLG
GN_MIX, GN_FFN, GN_PLE, GN_BS = 0, 8, 16, 24


def build():
    nc = bass.Bass("TRN2", target_bir_lowering=False)

    def din(name, shape):
        return nc.dram_tensor(name, shape, F32, kind="ExternalInput").ap()

    def dout(name, shape):
        return nc.dram_tensor(name, shape, F32, kind="ExternalOutput").ap()

    xp = din("xp", [2048, DM]); xs = din("xs", [128, DM])
    pp = din("pp", [2048, PLE]); psm = din("psm", [128, PLE])
    sC = din("sC", [16, 4, 128, 128]); snT = din("snT", [4, 128, 16]); smt = din("smt", [128, 4])
    w_in = din("w_in", [DM, INC]); w_out = din("w_out", [DM, DM])
    w_up = din("w_up", [DM, DFF]); w_down = din("w_down", [DFF, DM])
    w_pg = din("w_pg", [DM, DM]); w_pp = din("w_pp", [PLE, DM])
    gains = din("gains", [128, 32]); rowv = din("rowv", [RV_N]); nfin = din("nfin", [DM]); mlg = din("mlg", [512])
    wsg = din("wsg", [8, 128, 128])
    cmat = din("cmat", [12, 128, 128]); csm = din("csm", [128, 32])

    y_p = dout("y_p", [2048, DM]); y_s = dout("y_s", [128, DM])
    o_Cp = dout("o_Cp", [4, 128, 128]); o_Np = dout("o_Np", [4, 128]); o_Mp = dout("o_Mp", [1, 4])
    o_Vp = dout("o_Vp", [128, 512])
    o_Cs = dout("o_Cs", [16, 4, 128, 128]); o_NsT = dout("o_NsT", [4, 128, 16]); o_Ms = dout("o_Ms", [16, 4])
    o_Vs = dout("o_Vs", [128, 512])

    S = Sched(nc)
    from contextlib import ExitStack
    with ExitStack() as st:
        def sb(name, shape, dt=F32):
            return st.enter_context(nc.sbuf_tensor(name, shape, dt))

        def aps(*xs_):
            return [x for x in xs_ if not isinstance(x, (int, float)) and x is not None]

        def act(out, in_, func, bias=None, scale=None, accum=None):
            kw = {}
            if bias is not None:
                kw["bias"] = bias
            if scale is not None:
                kw["scale"] = scale
            if accum is not None:
                kw["accum_out"] = accum
            return S.op("act", lambda e: e.activation(out=out, in_=in_, func=func, **kw),
                        aps(in_, bias, scale), aps(out, accum))

        def tt(eng, out, in0, in1, op):
            return S.op(eng, lambda e: e.tensor_tensor(out=out, in0=in0, in1=in1, op=op), [in0, in1], [out])

        def ts(eng, out, in0, s1, op0, s2=None, op1=None, accum=None):
            kw = {}
            if op1 is not None:
                kw["op1"] = op1
            if accum is not None:
                kw["accum_out"] = accum
            return S.op(eng, lambda e: e.tensor_scalar(out=out, in0=in0, scalar1=s1, scalar2=s2, op0=op0, **kw),
                        aps(in0, s1, s2), aps(out, accum))

        def stt(eng, out, in0, scalar, in1, op0, op1, accum=None):
            kw = {}
            if accum is not None:
                kw["accum_out"] = accum
            return S.op(eng, lambda e: e.scalar_tensor_tensor(out=out, in0=in0, scalar=scalar, in1=in1,
                                                              op0=op0, op1=op1, **kw),
                        aps(in0, scalar, in1), aps(out, accum))

        def cp(eng, out, in_):
            if eng == "act":
                return act(out, in_, AF.Copy)
            return S.op(eng, lambda e: e.tensor_copy(out=out, in_=in_), [in_], [out])

        def bnstats(out, in_):
            return S.op("dve", lambda e: e.bn_stats(out=out, in_=in_), [in_], [out])

        def bnaggr(out, in_):
            return S.op("dve", lambda e: e.bn_aggr(out=out, in_=in_), [in_], [out])

        def redmax(out, in_):
            return S.op("dve", lambda e: e.tensor_reduce(out=out, in_=in_, axis=AX.X, op=ALU.max), [in_], [out])

        def recip(out, in_):
            return S.op("dve", lambda e: e.reciprocal(out=out, in_=in_), [in_], [out])

        def memset(eng, out, val):
            return S.op(eng, lambda e: e.memset(out, val), [], [out])

        CM = sb("CM", [128, 12, 128])
        CS = sb("CS", [128, 32])
        GN = sb("GN", [128, 32])
        MLGH = sb("MLGH", [128, 512])
        RV = sb("RV", [128, RV_N])
        IDB = sb("IDB", [128, 128], BF16)
        ONESB = sb("ONESB", [128, 128], BF16)
        SMALLB = sb("SMALLB", [128, 6, 32], BF16)
        WM = sb("WM", [128, 8, 128], BF16)
        NEGH = sb("NEGH", [128, 8])
        SMT = sb("SMT", [128, 4])
        MRUN = sb("MRUN", [128, 4])
        CST = sb("CST", [128, 4, 129])
        CSTB = sb("CSTB", [128, 4, 129], BF16)
        KTOKS = sb("KTOKS", [128, 4, 128], BF16)
        VWS = sb("VWS", [128, 4, 130], BF16)
        DECB = sb("DECBS", [128, 16, 4])
        NCOL = sb("NCOLS", [128, 4, 16])
        H = sb("H", [128, 9, DM])
        SMALL = sb("SMALL", [128, 6, 160])
        AR = sb("AR", [128, ARENA_F32])
        ARB = AR.bitcast(BF16)
        PB = [st.enter_context(nc.psum_tensor(f"pb{i}", [128, 512], F32)) for i in range(8)]
        PBB = [p.bitcast(BF16) for p in PB]

        class Arena:
            def __init__(self, off=0):
                self.off = off

            def f32(self, n, shape=None):
                assert self.off % 4 == 0
                o = self.off // 4
                self.off += n * 4
                assert self.off <= ARENA_F32 * 4, ("arena overflow", self.off)
                v = AR[:, o:o + n]
                return v if shape is None else v.rearrange(shape[0], **shape[1])

            def bf(self, n, shape=None):
                o = self.off // 2
                self.off += ((n * 2 + 3) // 4) * 4
                assert self.off <= ARENA_F32 * 4, ("arena overflow", self.off)
                v = ARB[:, o:o + n]
                return v if shape is None else v.rearrange(shape[0], **shape[1])

        class Banks:
            def __init__(self, ids):
                self.ids = list(ids)
                self.i = 0

            def next(self):
                b = self.ids[self.i % len(self.ids)]
                self.i += 1
                return b

        S.dma(H[:, 0, :], xp[0:128, :])
        S.dma(CM[:].rearrange("p k n -> p (k n)").rearrange("p (k n) -> p k n", k=12), cmat.rearrange("k p n -> p k n"))
        S.dma(CS[:], csm)
        S.dma(GN[:], gains)
        S.dma(RV[:], rowv.partition_broadcast(128))
        S.dma(SMT[:], smt)
        memset("pool", NEGH[:], -0.5)
        S.dma(MLGH[:], mlg.partition_broadcast(128))
        ts("dve", MLGH[:], MLGH[:], 0.5, ALU.mult)
        cp("dve", IDB[:], CM[:, CM_IDENT, :])
        cp("dve", ONESB[:], CM[:, CM_ONES, :])
        IDF = CM[:, CM_IDENT, :]
        ONESF = CM[:, CM_ONES, :]
        memset("pool", MRUN[:], 0.0)
        memset("pool", CST[:], 0.0)
        memset("pool", CSTB[:], 0.0)
        A0 = Arena(64 * 1024)
        wtmp = A0.f32(8 * 128, ("p (k n) -> p k n", dict(k=8)))
        wtb = A0.bf(8 * 128, ("p (k n) -> p k n", dict(k=8)))
        S.dma(wtmp, wsg.rearrange("k p n -> p k n"))
        for ty in range(2):
            tt("dve", wtb[:, ty * 4:(ty + 1) * 4, :], wtmp[:, ty * 4:(ty + 1) * 4, :],
               CM[:, CM_01 + ty, :].unsqueeze(1).to_broadcast([128, 4, 128]), ALU.mult)
        for k in range(8):
            S.tr(PBB[0][:, k * 128:(k + 1) * 128], wtb[:, k, :], IDB[:])
        cp("dve", WM[:], PBB[0][:, 0:1024].rearrange("p (k n) -> p k n", k=8))

        def load_cast(dst_bf, src_dram):
            S.dma(dst_bf, src_dram, eng="pool")

        def small(d, off, n):
            return SMALL[:, d, off:off + n]

        def bcast_rows(d, soff, vec, dg, ps_out):
            hl = SMALLB[:, d, soff:soff + 8]
            cp("dve", hl[:, 0:4], vec)
            tt("dve", hl[:, 4:8], vec, hl[:, 0:4], ALU.subtract)
            tt("pool", dg, IDB[:].unsqueeze(1).to_broadcast([128, 8, 128]),
               hl.unsqueeze(2).to_broadcast([128, 8, 128]), ALU.mult)
            S.mm(ps_out, [(ONESB[:], dg[:, 0:4, :].rearrange("p h s -> p (h s)")),
                          (ONESB[:], dg[:, 4:8, :].rearrange("p h s -> p (h s)"))])

        def rstd_from_ss(tmp, ss, n, inv_d, out):
            ts("pool", tmp, ss, inv_d, ALU.mult, EPS, ALU.add)
            tt("pool", out, tmp, NEGH[:, 0:n], ALU.pow)

        def norm_T(d, hsrc, a_bf, aT_dst, ssoff, goff, bank):
            ss = small(d, ssoff, 1)
            rs = small(d, ssoff + 1, 1)
            tmp = small(d, ssoff + 2, 1)
            memset("pool", ss, 0.0)
            act(a_bf, hsrc, AF.Square, accum=ss)
            rstd_from_ss(tmp, ss, 1, 1.0 / DM, rs)
            act(a_bf, hsrc, AF.Copy, scale=rs)
            S.mark("begin", bank)
            for kc in range(8):
                S.tr(PBB[bank][:, kc * 128:(kc + 1) * 128], a_bf[:, kc * 128:(kc + 1) * 128], IDB[:])
            tt("dve", aT_dst, PBB[bank][:, 0:1024].rearrange("p (k t) -> p k t", k=8),
               GN[:, goff:goff + 8].unsqueeze(2).to_broadcast([128, 8, 128]), ALU.mult)
            S.mark("end", bank)

        def run_pipeline(tiles, stages, background=None):
            n = len(tiles)
            for it in range(n + len(stages) - 1):
                streams = []
                for s_i in reversed(range(len(stages))):
                    ti = it - s_i
                    if 0 <= ti < n:
                        S.capture = []
                        for _ in stages[s_i](tiles[ti]):
                            pass
                        streams.append(S.capture)
                        S.capture = None
                busy = {}
                for stm in streams:
                    for o in stm:
                        if o[0] != "bank":
                            busy[o[1]] = busy.get(o[1], 0.0) + o[5]
                if background is not None and it in background and background[it] is not None:
                    streams.append(background[it])
                S.run_streams(streams)
                S.marks.append(("it", it, dict(S.free), busy))

        def ptile(i):
            return dict(ty=0, i=i, x=xp[i * 128:(i + 1) * 128, :], p=pp[i * 128:(i + 1) * 128, :],
                        y=y_p[i * 128:(i + 1) * 128, :])
        stile = dict(ty=1, i=0, x=xs, p=psm, y=y_s)
        SGS = [[ptile(i) for i in range(4)] + [stile] + [ptile(i) for i in range(4, 8)],
               [ptile(i) for i in range(8, 16)]]
        gidx = [0]

        R = Arena()
        R1_OFF = R.off
        WG = R.bf(8 * DM, ("p (k n) -> p k n", dict(k=8)))
        WP = R.bf(2 * DM, ("p (k n) -> p k n", dict(k=2)))
        NF = R.f32(DM)
        R2_OFF = R.off

        WA_OFF = ARENA_F32 * 4 - (8 * INC + 8 * DM) * 2
        WA = Arena(WA_OFF)
        WIN = WA.bf(8 * INC, ("p (k n) -> p k n", dict(k=8)))
        WOUT = WA.bf(8 * DM, ("p (k n) -> p k n", dict(k=8)))
        COLG = ((0, 512), (512, 512), (2560, 512), (1024, 512), (1536, 512), (2048, 512), (3072, 8))

        BWA = Arena(WA_OFF)
        WU = [BWA.bf(8 * 1024, ("p (k n) -> p k n", dict(k=8))), None]
        WD = [BWA.bf(8 * 1024, ("p (k n) -> p k n", dict(k=8))), None]
        WU[1] = BWA.bf(8 * 1024, ("p (k n) -> p k n", dict(k=8)))
        assert BWA.off <= WA_OFF + 8 * INC * 2
        WD[1] = Arena(R2_OFF + (8 * 9 * 128 + DM + 2 * 8 * 512) * 2).bf(8 * 1024, ("p (k n) -> p k n", dict(k=8)))

        def load_q(qd):
            wu, wd = WU[qd % 2], WD[qd % 2]
            w_up_v = w_up.rearrange("(k p) n -> p k n", p=128)
            for hf in range(2):
                c0 = qd * 1024 + hf * 512
                load_cast(wu[:, :, hf * 512:(hf + 1) * 512], w_up_v[:, :, c0:c0 + 512])
            for hf in range(2):
                r0 = qd * 1024 + hf * 512
                load_cast(wd[:, hf * 4:(hf + 1) * 4, :],
                          w_down[r0:r0 + 512, :].rearrange("(f p) n -> p f n", p=128))

        def load_phaseA_weights(by_group):
            S.capture = []
            w_in_v = w_in.rearrange("(k p) n -> p k n", p=128)
            for gi_, (c0, n) in enumerate(COLG):
                load_cast(WIN[:, :, c0:c0 + n], w_in_v[:, :, c0:c0 + n])
                if by_group and gi_ == 2:
                    for slot_i in range(1, 9):
                        S.dma(H[:, slot_i, :], SGS[0][slot_i]["x"], eng="pool")
            load_cast(WOUT, w_out.rearrange("(k p) n -> p k n", p=128))
            stream = S.capture
            S.capture = None
            return stream

        for sg_i, sg in enumerate(SGS):
            for slot_i, t in enumerate(sg):
                t["slot"] = slot_i
                t["g"] = gidx[0]
                gidx[0] += 1
                if sg_i > 0:
                    S.dma(H[:, slot_i, :], t["x"])

            A = Arena(0)
            AT = [A.bf(8 * 128, ("p (k n) -> p k n", dict(k=8))) for _ in range(2)]
            A_BF = A.bf(DM)
            U = [A.bf(512) for _ in range(2)]
            VN = [A.f32(512) for _ in range(2)]
            VNB = A.bf(512)
            TH = A.f32(512)
            QTOK = [A.bf(512) for _ in range(2)]
            YGM = A.bf(512)
            DGA = A.bf(1024, ("p (h s) -> p h s", dict(h=8)))
            PBIG = A.f32(512, ("p (h s) -> p h s", dict(h=4)))
            QT = [A.bf(512, ("p (h t) -> p h t", dict(h=4))) for _ in range(3)]
            KT = [A.bf(512, ("p (h t) -> p h t", dict(h=4))) for _ in range(3)]
            KTOK = [A.bf(512, ("p (h d) -> p h d", dict(h=4))) for _ in range(4)]
            VAUG = [A.bf(4 * 130, ("p (h e) -> p h e", dict(h=4)))[:, :, 0:129] for _ in range(4)]
            SGM = [A.bf(512, ("p (h e) -> p h e", dict(h=4))) for _ in range(5)]
            YT = [A.bf(8 * 128, ("p (k t) -> p k t", dict(k=8))) for _ in range(4)]
            NUM = [A.f32(4 * 129, ("p (h e) -> p h e", dict(h=4))) for _ in range(2)]
            DGC = A.bf(1024, ("p (h s) -> p h s", dict(h=8)))
            BIG1 = A.f32(512, ("p (h s) -> p h s", dict(h=4)))
            STB = A.bf(512, ("p (h s) -> p h s", dict(h=4)))
            QCS = A.f32(4 * 129, ("p (h e) -> p h e", dict(h=4)))
            VW = A.bf(4 * 130, ("p (h e) -> p h e", dict(h=4)))[:, :, 0:129]
            YML = A.bf(512)
            JUNK = A.f32(128)
            has_s = any(t["ty"] == 1 for t in sg)
            if has_s:
                CSB3 = [A.bf(16 * 130, ("p (j e) -> p j e", dict(j=16)))[:, :, 0:129] for _ in range(2)]
                QTD = A.bf(16 * 128, ("p (j t) -> p j t", dict(j=16)))
                DECS = A.f32(64, ("p (j h) -> p j h", dict(j=16)))
            assert A.off <= WA_OFF, ("phase A work buffers overflow into weights", A.off, WA_OFF)
            bgA = load_phaseA_weights(True) if sg_i == 0 else None
            if has_s:
                memset("pool", QTD, 0.0)
            for v_ in VAUG:
                memset("pool", v_[:, :, 128:129], 1.0)

            bP = Banks([1, 2, 3])
            bG = Banks([0])
            bP1 = Banks([4])
            bM1 = Banks([5, 6])
            bM2 = Banks([7])

            def hv(bx, by, h):
                b_ = PB[bx] if h % 2 == 0 else PB[by]
                return b_[:, (h // 2) * 129:(h // 2) * 129 + 129]

            def stage_N(t):
                g = t["g"]
                norm_T(g % 6, H[:, t["slot"], :], A_BF, AT[g % 2], 0, GN_MIX, 7)
                yield

            def stage_P(t):
                ty = t["ty"]; g = t["g"]; d2 = g % 2; d3 = g % 3; d4 = g % 4
                sm = lambda off, n: small(g % 6, off, n)
                at = AT[d2]
                lhs = [at[:, kc, :] for kc in range(8)]

                def proj(col0, n, bank):
                    ps = PB[bank][:, 0:n]
                    S.mm(ps, [(lhs[kc], WIN[:, kc, col0:col0 + n]) for kc in range(8)])
                    return ps
                b_u, b_v, b_o = bP.next(), bP.next(), bP.next()
                ps_u = proj(0, 512, b_u)
                ps_v = proj(512, 512, b_v)
                ps_o = proj(2560, 512, b_o)
                act(U[d2], ps_u, AF.Gelu_apprx_tanh)
                act(VN[d2], ps_v, AF.Gelu_apprx_tanh)
                act(TH, ps_o, AF.Tanh, scale=0.5)
                yield
                stt("dve", SGM[g % 5], TH.rearrange("p (h e) -> p h e", h=4), 1.0,
                    MLGH[:].rearrange("p (h e) -> p h e", h=4), ALU.add, ALU.mult)
                ps_q = proj(1024, 512, bP.next())
                cp("act", QTOK[d2], ps_q)
                yield
                ps_k = proj(1536, 512, bP.next())
                act(KTOK[g % 4], ps_k.rearrange("p (h d) -> p h d", h=4), AF.Copy, scale=float(128 ** -0.5))
                yield
                ps_vm = proj(2048, 512, bP.next())
                cp("dve", VAUG[g % 4][:, :, 0:128], ps_vm.rearrange("p (h d) -> p h d", h=4))
                yield
                bg = bP.next()
                ps_g = proj(3072, 8, bg)
                zg = sm(4, 8)
                cp("dve", zg, ps_g)
                yield

            def stage_P1(t):
                ty = t["ty"]; g = t["g"]; d2 = g % 2; d3 = g % 3; d4 = g % 4
                sm = lambda off, n: small(g % 6, off, n)
                zg = sm(4, 8)
                bP = bP1
                U_, VN_ = U[d2], VN[d2]
                bt = bP.next()
                for h in range(4):
                    S.tr(PBB[bt][:, h * 128:(h + 1) * 128], QTOK[d2][:, h * 128:(h + 1) * 128], IDB[:])
                for h in range(4):
                    S.tr(PBB[bt][:, 512 + h * 128:512 + (h + 1) * 128], KTOK[g % 4][:, h, :], IDB[:])
                cp("act", QT[g % 3], PBB[bt][:, 0:512].rearrange("p (h t) -> p h t", h=4))
                cp("dve", KT[g % 3], PBB[bt][:, 512:1024].rearrange("p (h t) -> p h t", h=4))
                yield
                st6 = sm(12, 6); mv = sm(18, 2); lrs = sm(20, 1); ltmp = sm(21, 1)
                bnstats(st6, VN_)
                bnaggr(mv, st6)
                rstd_from_ss(ltmp, mv[:, 1:2], 1, 1.0, lrs)
                ts("dve", VN_, VN_, mv[:, 0:1], ALU.subtract, lrs, ALU.mult)
                yield
                tt("pool", VN_, VN_, RV[:, RV_LNG:RV_LNG + 512], ALU.mult)
                tt("pool", VN_, VN_, RV[:, RV_LNB:RV_LNB + 512], ALU.add)
                cp("pool", VNB, VN_)
                if ty == 1:
                    S.dma(o_Vs, VN_)
                elif t["i"] == NPT - 1:
                    S.dma(o_Vp, VN_)
                yield
                bs_ = bP.next()
                ps_s = PB[bs_][:, :]
                for h in range(4):
                    S.mm(ps_s[:, h * 128:(h + 1) * 128], [(WM[:, ty * 4 + h, :], VNB[:, h * 128:(h + 1) * 128])])
                for h in range(4):
                    stt("dve", YGM[:, h * 128:(h + 1) * 128], ps_s[:, h * 128:(h + 1) * 128],
                        GN[:, GN_BS + ty * 4 + h:GN_BS + ty * 4 + h + 1], U_[:, h * 128:(h + 1) * 128], ALU.add, ALU.mult)
                yield
                by = bP.next()
                for h in range(4):
                    S.tr(PBB[by][:, h * 128:(h + 1) * 128], YGM[:, h * 128:(h + 1) * 128], IDB[:])
                cp("act", YT[g % 4][:, 0:4, :], PBB[by][:, 0:512].rearrange("p (k t) -> p k t", k=4))
                yield

            def stage_G(t):
                ty = t["ty"]; g = t["g"]
                sm = lambda off, n: small(g % 6, off, n)
                zg = sm(4, 8)
                bP = bG
                ig = sm(24, 4); e1 = sm(28, 4); l1 = sm(32, 4); bb = sm(36, 4); aa = sm(44, 4)
                mx = sm(48, 4); xf = sm(120, 4)
                tt("dve", ig, zg[:, 0:4], RV[:, RV_BI:RV_BI + 4], ALU.add)
                tt("dve", xf, zg[:, 4:8], RV[:, RV_BF:RV_BF + 4], ALU.add)
                act(e1, xf, AF.Exp, scale=-1.0)
                act(l1, e1, AF.Ln, bias=1.0)
                bc = bP.next()
                ps_c = PB[bc][:, 0:4]
                S.mm(ps_c, [(CM[:, CM_CUM + ty, :], l1)])
                ts("dve", bb, ps_c, -1.0, ALU.mult)
                tt("dve", aa, ig, bb, ALU.subtract)
                yield
                ba = bP.next()
                ps_A = PB[ba][:, :]
                bcast_rows(g % 6, 0, aa, DGA, ps_A)
                tt("dve", PBIG, ps_A.rearrange("p (h s) -> p h s", h=4),
                   CM[:, CM_MASK + ty, :].unsqueeze(1).to_broadcast([128, 4, 128]), ALU.add)
                redmax(mx, PBIG)
                tt("dve", mx, mx, bb, ALU.add)
                if ty == 1:
                    S.dma(NCOL[:], snT.rearrange("h d j -> d h j"))
                    for h in range(2):
                        S.dma(CSB3[h][:, :, 0:128], sC[:, h, :, :].rearrange("j d e -> d j e"), eng="pool")
                yield

            def stage_M1(t):
                ty = t["ty"]; g = t["g"]; d2 = g % 2; d3 = g % 3
                sm = lambda off, n: small(g % 6, off, n)
                bb = sm(36, 4); gg = sm(40, 4); aa = sm(44, 4); mx = sm(48, 4); mt = sm(52, 4); cc = sm(56, 4)
                wint = sm(60, 4); negm = sm(64, 4); emt = sm(68, 4); mb8 = sm(72, 8); lsel = sm(80, 8)
                wend = sm(88, 4); dec = sm(92, 4); tmp4 = sm(124, 4); tmp5 = sm(128, 4)
                mprev = MRUN[:] if ty == 0 else SMT[:]
                tt("dve", gg, bb, mprev, ALU.add)
                tt("dve", mt, mx, gg, ALU.max)
                tt("dve", cc, bb, mt, ALU.subtract)
                cp("dve", mb8[:, 0:4], mt)
                cp("dve", mb8[:, 4:8], bb)
                bl = bM1.next()
                ps_l = PB[bl][:, 0:8]
                S.mm(ps_l, [(CM[:, CM_SEL + ty, :], mb8)])
                cp("dve", lsel, ps_l)
                if ty == 0:
                    yield
                ts("dve", negm, mt, -1.0, ALU.mult)
                tt("dve", tmp4, gg, mt, ALU.subtract)
                act(wint, tmp4, AF.Exp)
                act(emt, negm, AF.Exp)
                yield
                bC = bM1.next()
                ps_C = PB[bC][:, :]
                bcast_rows(g % 6, 8, cc, DGC, ps_C)
                tt("dve", BIG1, ps_C.rearrange("p (h s) -> p h s", h=4),
                   CM[:, CM_MASKT + ty, :].unsqueeze(1).to_broadcast([128, 4, 128]), ALU.add)
                yield
                for h in range(4):
                    act(BIG1[:, h, :], BIG1[:, h, :], AF.Exp, bias=aa[:, h:h + 1])
                bsc = bM1.next()
                ps_sc = PB[bsc][:, :].rearrange("p (h t) -> p h t", h=4)
                for h in range(4):
                    S.mm(ps_sc[:, h, :], [(KT[g % 3][:, h, :], QT[g % 3][:, h, :])])
                tt("dve", STB, ps_sc, BIG1, ALU.mult)
                yield
                bx, by = bM1.next(), bM1.next()
                for h in range(4):
                    S.mm(hv(bx, by, h), [(STB[:, h, :], VAUG[g % 4][:, h, :])])
                tt("dve", tmp4, aa, lsel[:, 4:8], ALU.add)
                tt("dve", tmp4, tmp4, lsel[:, 0:4], ALU.subtract)
                act(wend, tmp4, AF.Exp)
                tt("dve", tmp5, lsel[:, 4:8], mprev, ALU.add)
                tt("dve", tmp5, tmp5, lsel[:, 0:4], ALU.subtract)
                act(dec, tmp5, AF.Exp)
                vw_t = VW if ty == 0 else VWS[:, :, 0:129]
                tt("pool", vw_t, VAUG[g % 4], wend.unsqueeze(2).to_broadcast([128, 4, 129]), ALU.mult)
                if ty == 1:
                    cp("pool", KTOKS[:], KTOK[g % 4])
                yield
                num = NUM[d2]
                if ty == 0:
                    cx, cy = bM1.next(), bM1.next()
                    for half, bk in enumerate((bx, by)):
                        cp("dve", num.rearrange("p (a b) e -> p a b e", b=2)[:, :, half, :],
                           PB[bk][:, 0:258].rearrange("p (h e) -> p h e", h=2))
                    for h in range(4):
                        S.mm(hv(cx, cy, h), [(QT[g % 3][:, h, :], CSTB[:, h, :])])
                    for h in range(4):
                        act(QCS[:, h, :], hv(cx, cy, h), AF.Copy, scale=wint[:, h:h + 1])
                    tt("dve", num, num, QCS, ALU.add)
                    yield
                    ux, uy = bM1.next(), bM1.next()
                    for h in range(4):
                        S.mm(hv(ux, uy, h), [(KTOK[g % 4][:, h, :], VW[:, h, :])])
                    for h in range(4):
                        stt("dve", CST[:, h, :], CST[:, h, :], dec[:, h:h + 1], hv(ux, uy, h), ALU.mult, ALU.add)
                    cp("pool", CSTB[:], CST[:])
                    cp("dve", MRUN[:], lsel[:, 0:4])
                    if t["i"] == NPT - 1:
                        for h in range(4):
                            S.dma(o_Cp[h], CST[:, h, 0:128])
                        S.dma(o_Np.rearrange("h d -> d h"), CST[:, :, 128], allow_slow_non_contiguous=True)
                        S.dma(o_Mp, lsel[0:1, 0:4])
                    yield
                else:
                    for half, bk in enumerate((bx, by)):
                        cp("dve", num.rearrange("p (a b) e -> p a b e", b=2)[:, :, half, :],
                           PB[bk][:, 0:258].rearrange("p (h e) -> p h e", h=2))
                    tt("dve", DECS, dec.unsqueeze(1).to_broadcast([128, 16, 4]),
                       CS[:, 0:16].unsqueeze(2).to_broadcast([128, 16, 4]), ALU.mult)
                    bd = bM1.next()
                    ps_d = PB[bd][:, 0:64]
                    S.mm(ps_d, [(ONESF, DECS.rearrange("p j h -> p (j h)"))])
                    cp("dve", DECB[:].rearrange("p j h -> p (j h)"), ps_d)
                    pst = list(mt.ap[0])[0]
                    S.dma(o_Ms, bass.AP(mt.tensor, mt.offset + 7 * pst, [[8 * pst, 16], [1, 4]]))
                    yield
                    for h in range(4):
                        CSB = CSB3[h % 2]
                        if h >= 2:
                            S.dma(CSB[:, :, 0:128], sC[:, h, :, :].rearrange("j d e -> d j e"), eng="pool")
                        cp("dve", CSB[:, :, 128], NCOL[:, h, :])
                        dst = bass.AP(QTD.tensor, QTD.offset, [list(QTD.ap[0]), [128 + 8, 16], [1, 8]])
                        cp("dve", dst, QT[g % 3][:, h, :].rearrange("p (j t) -> p j t", j=16))
                        bq = bM1.next()
                        pq = PB[bq][:, 0:129]
                        S.mm(pq, [(QTD[:, j, :], CSB[:, j, :]) for j in range(16)])
                        act(QCS[:, h, :], pq, AF.Copy, scale=wint[:, h:h + 1])
                        tt("dve", num[:, h, :], num[:, h, :], QCS[:, h, :], ALU.add)
                        yield

            def stage_M2(t):
                g = t["g"]; d2 = g % 2; d3 = g % 3
                sm = lambda off, n: small(g % 6, off, n)
                emt = sm(68, 4); aden = sm(96, 4); rden = sm(100, 4)
                ssq = sm(104, 4); r2 = sm(108, 4); rs2 = sm(112, 4); scl = sm(116, 4); rtmp = sm(132, 4)
                num = NUM[d2]
                den = num[:, :, 128]
                stt("dve", aden, den, -1.0, den, ALU.mult, ALU.max)
                tt("dve", aden, aden, emt, ALU.max)
                recip(rden, aden)
                memset("pool", ssq, 0.0)
                for h in range(4):
                    act(JUNK, num[:, h, 0:128], AF.Square, accum=ssq[:, h:h + 1])
                yield
                tt("dve", r2, rden, rden, ALU.mult)
                tt("dve", r2, r2, ssq, ALU.mult)
                rstd_from_ss(rtmp, r2, 4, 1.0 / 128, rs2)
                tt("dve", scl, rden, rs2, ALU.mult)
                yield
                for h in range(4):
                    stt("dve", YML[:, h * 128:(h + 1) * 128], num[:, h, 0:128], scl[:, h:h + 1], SGM[g % 5][:, h, :],
                        ALU.mult, ALU.mult)
                yield
                b7 = bM2.next()
                S.mark("begin", b7)
                for h in range(4):
                    S.tr(PBB[b7][:, h * 128:(h + 1) * 128], YML[:, h * 128:(h + 1) * 128], IDB[:])
                cp("act", YT[g % 4][:, 4:8, :], PBB[b7][:, 0:512].rearrange("p (k t) -> p k t", k=4))
                S.mark("end", b7)
                yield
                for n in range(2):
                    bh = bM2.next()
                    S.mark("begin", bh)
                    ps_h = PB[bh][:, :]
                    S.mm(ps_h, [(YT[g % 4][:, c, :], WOUT[:, c, n * 512:(n + 1) * 512]) for c in range(8)])
                    hs = H[:, t["slot"], n * 512:(n + 1) * 512]
                    tt("dve", hs, hs, ps_h, ALU.add)
                    S.mark("end", bh)
                    yield

            S.marks.append(("A start", sg_i, max(S.free.values())))
            S.capture = []
            load_q(0)
            bgQ0 = S.capture
            S.capture = None
            run_pipeline(sg, [stage_N, stage_P, stage_P1, stage_G, stage_M1, stage_M2],
                         background={0: bgA, len(sg) + 1: bgQ0})
            S.marks.append(("A end", sg_i, max(S.free.values())))

            B = Arena(R2_OFF)
            nt = len(sg)
            NTOK = nt * 128
            A2T = B.bf(8 * 9 * 128, ("p (k n) -> p k n", dict(k=8)))
            ABF = B.bf(DM)
            GT = [B.bf(8 * 512, ("p (k n) -> p k n", dict(k=8))) for _ in range(2)]
            wd1_chk = B.bf(8 * 1024)
            assert wd1_chk.offset == WD[1].offset, (wd1_chk.offset, WD[1].offset)
            RL = [B.f32(512) for _ in range(2)]
            assert B.off <= WA_OFF, ("phase B low region overflows", B.off, WA_OFF)
            B = Arena(WA_OFF + 8 * INC * 2)
            if has_s:
                CSF2 = [B.f32(8 * 129, ("p (j e) -> p j e", dict(j=8))) for _ in range(3)]
                KMK2 = [B.bf(8 * 128, ("p (j d) -> p j d", dict(j=8)))]
                NOUT = B.f32(64, ("p (h j) -> p h j", dict(h=4)))

            def sample_state_update():
                bU = Banks([6, 7])

                def load_chunk(c):
                    h_, j0 = c // 2, (c % 2) * 8
                    S.dma(CSF2[c % 3][:, :, 0:128], sC[j0:j0 + 8, h_, :, :].rearrange("j d e -> d j e"))
                load_chunk(0)
                load_chunk(1)
                load_chunk(2)
                for h in range(4):
                    for half in range(2):
                        c = h * 2 + half
                        j0 = half * 8
                        csf = CSF2[c % 3]
                        kmk = KMK2[0]
                        cp("dve", csf[:, :, 128], NCOL[:, h, j0:j0 + 8])
                        tt("pool", kmk, KTOKS[:, h, :].unsqueeze(1).to_broadcast([128, 8, 128]),
                           CS[:, 16 + j0:24 + j0].unsqueeze(2).to_broadcast([128, 8, 128]), ALU.mult)
                        for grp3 in ((0, 1, 2), (3, 4, 5), (6, 7)):
                            bj = bU.next()
                            for k3, jj in enumerate(grp3):
                                S.mm(PB[bj][:, k3 * 129:(k3 + 1) * 129], [(kmk[:, jj, :], VWS[:, h, 0:129])])
                            for k3, jj in enumerate(grp3):
                                stt("dve", csf[:, jj, :], csf[:, jj, :], DECB[:, j0 + jj, h:h + 1],
                                    PB[bj][:, k3 * 129:(k3 + 1) * 129], ALU.mult, ALU.add)
                        S.dma(o_Cs[j0:j0 + 8, h, :, :].rearrange("j d e -> d j e"), csf[:, :, 0:128])
                        cp("dve", NOUT[:, h, j0:j0 + 8], csf[:, :, 128])
                        if c + 3 < 8:
                            load_chunk(c + 3)
                S.dma(o_NsT.rearrange("h d j -> d h j"), NOUT)

            S.capture = []

            def norm_b(t):
                norm_T(t["slot"] % 5, H[:, t["slot"], :], ABF,
                       A2T[:, :, t["slot"] * 128:(t["slot"] + 1) * 128], 136, GN_FFN, 6 + t["slot"] % 2)

            S.capture = None
            GSZ = 384 if NTOK % 384 == 0 else 512
            for t in sg[0:GSZ // 128]:
                norm_b(t)
            late_norm = list(sg[GSZ // 128:])
            S.capture = []
            load_cast(WG, w_pg.rearrange("(k p) n -> p k n", p=128))
            load_cast(WP, w_pp.rearrange("(k p) n -> p k n", p=128))
            S.dma(NF, nfin.partition_broadcast(128))
            tgroups = [(g0, min(g0 + GSZ, NTOK)) for g0 in range(0, NTOK, GSZ)]
            gi_box = [0]

            def emit_up(qd, g0, g1):
                wu = WU[qd % 2]
                n = g1 - g0
                gt = GT[gi_box[0] % 2]
                gi_box[0] += 1
                for fc in range(8):
                    ps = PB[2 + fc % 4][:, 0:n]
                    S.mm(ps, [(wu[:, kc, fc * 128:(fc + 1) * 128], A2T[:, kc, g0:g1]) for kc in range(8)])
                    rl = RL[fc % 2]
                    act(rl[:, 0:n], ps, AF.Relu)
                    tt("dve", gt[:, fc, 0:n], rl[:, 0:n], rl[:, 0:n], ALU.mult)
                return gt

            def emit_down(qd, g0, g1, gt):
                wd = WD[qd % 2]
                for ti in range(g0 // 128, g1 // 128):
                    lt = slice(ti * 128 - g0, ti * 128 - g0 + 128)
                    for nn in range(2):
                        ps_h = PB[nn][:, :]
                        S.mm(ps_h, [(gt[:, fc, lt], wd[:, fc, nn * 512:(nn + 1) * 512]) for fc in range(8)])
                        hs = H[:, ti, nn * 512:(nn + 1) * 512]
                        tt("dve", hs, hs, ps_h, ALU.add)

            def emit_group(qd, g0, g1):
                wu, wd = WU[qd % 2], WD[qd % 2]
                n = g1 - g0
                gt = GT[gi_box[0] % 2]
                gi_box[0] += 1
                for fc in range(8):
                    ps = PB[2 + fc % 4][:, 0:n]
                    S.mm(ps, [(wu[:, kc, fc * 128:(fc + 1) * 128], A2T[:, kc, g0:g1]) for kc in range(8)])
                    rl = RL[fc % 2]
                    act(rl[:, 0:n], ps, AF.Relu)
                    tt("dve", gt[:, fc, 0:n], rl[:, 0:n], rl[:, 0:n], ALU.mult)
                for ti in range(g0 // 128, g1 // 128):
                    lt = slice(ti * 128 - g0, ti * 128 - g0 + 128)
                    for nn in range(2):
                        ps_h = PB[nn][:, :]
                        S.mm(ps_h, [(gt[:, fc, lt], wd[:, fc, nn * 512:(nn + 1) * 512]) for fc in range(8)])
                        hs = H[:, ti, nn * 512:(nn + 1) * 512]
                        tt("dve", hs, hs, ps_h, ALU.add)

            load_q(1)
            emit_group(0, *tgroups[0])
            x_stream = S.capture
            S.capture = []
            for t in late_norm:
                norm_b(t)
            y_stream = S.capture
            S.capture = None
            S.run_streams([x_stream, y_stream])

            S.capture = []
            seq = [(qd, gi_, g0, g1) for qd in range(4) for gi_, (g0, g1) in enumerate(tgroups)
                   if not (qd == 0 and gi_ == 0)]
            pend_ = None
            for k_ in range(len(seq) + 1):
                if k_ < len(seq):
                    qd, gi_, g0, g1 = seq[k_]
                    gt_new = emit_up(qd, g0, g1)
                if pend_ is not None:
                    pq, pgi, pg0, pg1, pgt = pend_
                    emit_down(pq, pg0, pg1, pgt)
                    if pgi == len(tgroups) - 1 and pq + 2 < 4:
                        load_q(pq + 2)
                pend_ = (qd, gi_, g0, g1, gt_new) if k_ < len(seq) else None
            b_stream = S.capture
            S.capture = None
            streams_b = [b_stream]
            if has_s:
                S.capture = []
                sample_state_update()
                streams_b.append(S.capture)
                S.capture = None
            S.run_streams(streams_b)

            bgC = load_phaseA_weights(False) if sg_i + 1 < len(SGS) else None
            C = Arena(R2_OFF)
            A3 = [C.bf(DM) for _ in range(2)]
            A3T = [C.bf(8 * 128, ("p (k n) -> p k n", dict(k=8))) for _ in range(2)]
            PF = [C.f32(PLE) for _ in range(2)]
            PBF = [C.bf(PLE) for _ in range(2)]
            PT = [C.bf(2 * 128, ("p (k n) -> p k n", dict(k=2))) for _ in range(2)]
            THC = [C.f32(DM) for _ in range(2)]
            TC = [C.f32(DM) for _ in range(3)]
            YO = [C.f32(DM) for _ in range(2)]
            JK = C.bf(DM)
            assert C.off <= WA_OFF, ("phase C overlaps prefetched weights", C.off, WA_OFF)
            bC0 = Banks([0, 1])
            bC1 = Banks([2, 3, 4, 5])

            def stage_C0(t):
                par = t["slot"] % 2
                d5 = t["slot"] % 5
                hsl = H[:, t["slot"], :]
                S.dma(PF[par], t["p"])
                ss = small(d5, 140, 1); rs = small(d5, 141, 1); tmp = small(d5, 142, 1)
                memset("pool", ss, 0.0)
                act(A3[par], hsl, AF.Square, accum=ss)
                rstd_from_ss(tmp, ss, 1, 1.0 / DM, rs)
                act(A3[par], hsl, AF.Copy, scale=rs)
                cp("pool", PBF[par], PF[par])
                yield

            def stage_C0b(t):
                par = t["slot"] % 2
                bn_ = bC0.next()
                for kc in range(8):
                    S.tr(PBB[bn_][:, kc * 128:(kc + 1) * 128], A3[par][:, kc * 128:(kc + 1) * 128], IDB[:])
                tt("dve", A3T[par], PBB[bn_][:, 0:1024].rearrange("p (k t) -> p k t", k=8),
                   GN[:, GN_PLE:GN_PLE + 8].unsqueeze(2).to_broadcast([128, 8, 128]), ALU.mult)
                bt = bC0.next()
                for c in range(2):
                    S.tr(PBB[bt][:, c * 128:(c + 1) * 128], PBF[par][:, c * 128:(c + 1) * 128], IDB[:])
                cp("act", PT[par], PBB[bt][:, 0:256].rearrange("p (k t) -> p k t", k=2))
                yield

            def stage_C1(t):
                par = t["slot"] % 2
                p3 = t["slot"] % 3
                for nn in range(2):
                    ps_g = PB[bC1.next()][:, :]
                    S.mm(ps_g, [(A3T[par][:, kc, :], WG[:, kc, nn * 512:(nn + 1) * 512]) for kc in range(8)])
                    act(THC[par][:, nn * 512:(nn + 1) * 512], ps_g, AF.Tanh, scale=0.5)
                    ps_p = PB[bC1.next()][:, :]
                    S.mm(ps_p, [(PT[par][:, c, :], WP[:, c, nn * 512:(nn + 1) * 512]) for c in range(2)])
                    stt("dve", TC[p3][:, nn * 512:(nn + 1) * 512], THC[par][:, nn * 512:(nn + 1) * 512], 1.0, ps_p,
                        ALU.add, ALU.mult)
                    yield

            def stage_C2(t):
                p3 = t["slot"] % 3
                d5 = t["slot"] % 5
                hsl = H[:, t["slot"], :]
                stt("dve", hsl, TC[p3], 0.5, hsl, ALU.mult, ALU.add)
                ss = small(d5, 144, 1)
                memset("pool", ss, 0.0)
                act(JK, hsl, AF.Square, accum=ss)
                yield

            def stage_C3(t):
                par = t["slot"] % 2
                d5 = t["slot"] % 5
                hsl = H[:, t["slot"], :]
                ss = small(d5, 144, 1); rs = small(d5, 145, 1); tmp = small(d5, 146, 1)
                rstd_from_ss(tmp, ss, 1, 1.0 / DM, rs)
                stt("dve", YO[par], hsl, rs, NF, ALU.mult, ALU.mult)
                S.dma(t["y"], YO[par])
                yield

            S.marks.append(("B end", sg_i, max(S.free.values())))
            run_pipeline(sg, [stage_C0, stage_C0b, stage_C1, stage_C2, stage_C3], background={0: bgC})
            S.marks.append(("C end", sg_i, max(S.free.values())))

        S.finish()
    return nc, S


_CACHE = {}


def _consts():
    idx = np.arange(128)
    seq = idx // 8
    ident = np.eye(128, dtype=np.float32)
    ones = np.ones((128, 128), np.float32)
    causal_ts = (idx[None, :] <= idx[:, None])
    same = (seq[:, None] == seq[None, :])
    cum_P = causal_ts.T.astype(np.float32)
    cum_S = (causal_ts.T & same).astype(np.float32)
    mask_P = np.where(causal_ts, 0.0, NEG).astype(np.float32)
    mask_S = np.where(causal_ts & same, 0.0, NEG).astype(np.float32)
    maskT_P = mask_P.T.copy()
    maskT_S = mask_S.T.copy()
    sel_P = np.zeros((128, 128), np.float32); sel_P[127, :] = 1.0
    sel_S = (idx[:, None] == (seq[None, :] * 8 + 7)).astype(np.float32)
    t01_P = causal_ts.astype(np.float32)
    bd01_S = (causal_ts & same).astype(np.float32)
    cmat = np.stack([ident, ones, cum_P, cum_S, mask_P, mask_S, maskT_P, maskT_S, sel_P, sel_S, t01_P, bd01_S])
    lastmask = (idx[:, None] == (np.arange(16)[None, :] * 8 + 7)).astype(np.float32)
    seqmask = (seq[:, None] == np.arange(16)[None, :]).astype(np.float32)
    csm = np.concatenate([lastmask, seqmask], axis=1)
    return np.ascontiguousarray(cmat), np.ascontiguousarray(csm)


def kernel(x_prompt, x_sample, p_prompt, p_sample, state_C, state_n, state_m,
           norm_mix, w_in, gm_ln_g, gm_ln_b, gm_ws, gm_bs, ml_b_i, ml_b_f, ml_norm,
           w_out, norm_ffn, w_up, w_down, norm_ple, w_ple_gate, w_ple_proj, norm_final):
    f = lambda a: np.ascontiguousarray(np.asarray(a, dtype=np.float32))
    x_prompt, x_sample, p_prompt, p_sample = f(x_prompt), f(x_sample), f(p_prompt), f(p_sample)
    state_C, state_n, state_m = f(state_C), f(state_n), f(state_m)
    if "nc" not in _CACHE:
        _CACHE["nc"] = build()[0]
    nc = _CACHE["nc"]
    cmat, csm = _consts()
    gbs = f(gm_bs)[0]
    gains = np.concatenate([f(norm_mix)[0].reshape(8, 128).T, f(norm_ffn)[0].reshape(8, 128).T,
                            f(norm_ple)[0].reshape(8, 128).T, gbs.T, np.tile(gbs[:, :8], (1, 16)).T], axis=1)
    rowv = np.concatenate([f(gm_ln_g)[0], f(gm_ln_b)[0], f(ml_b_i)[0], f(ml_b_f)[0]])
    gws = f(gm_ws)[0]
    wsg = np.concatenate([gws, np.tile(gws[:, :8, :8], (1, 16, 16))], axis=0)
    shared = dict(w_in=f(w_in)[0], w_out=f(w_out)[0], w_up=f(w_up)[0], w_down=f(w_down)[0],
                  w_pg=f(w_ple_gate)[0], w_pp=f(w_ple_proj)[0], gains=f(gains), rowv=f(rowv),
                  nfin=f(norm_final), mlg=f(ml_norm)[0], wsg=f(wsg), cmat=cmat, csm=csm)
    in_maps = []
    for c in range(NCORES):
        sl = slice(16 * c, 16 * (c + 1))
        m = dict(shared)
        m.update(xp=x_prompt[c], xs=x_sample[sl].reshape(128, DM), pp=p_prompt[0, c],
                 psm=p_sample[0, sl].reshape(128, PLE), sC=state_C[0, sl],
                 snT=np.ascontiguousarray(state_n[0, sl].transpose(1, 2, 0)),
                 smt=np.ascontiguousarray(np.repeat(state_m[0, sl], 8, axis=0)))
        in_maps.append(m)
    res = run_bass_kernel_spmd(nc, in_maps, core_ids=list(range(NCORES)))
    R = res.results
    g = lambda k: [np.asarray(R[c][k], dtype=np.float32) for c in range(NCORES)]
    y_prompt = np.stack(g("y_p"))
    y_sample = np.concatenate(g("y_s")).reshape(128, 8, DM)
    Cp = np.stack(g("o_Cp"))[None]
    Np_ = np.stack(g("o_Np"))[None]
    Mp = np.stack([a.reshape(4) for a in g("o_Mp")])[None]
    Vp = np.stack(g("o_Vp"))[None]
    Cs = np.concatenate(g("o_Cs"))[None]
    Ns = np.concatenate([a.transpose(2, 0, 1) for a in g("o_NsT")])[None]
    Ms = np.concatenate(g("o_Ms"))[None]
    Vs = np.concatenate(g("o_Vs")).reshape(128, 8, 512)[None]
    return (y_prompt, y_sample, Cp, Np_, Mp, Vp, Cs, Ns, Ms, Vs)
```

```python
import numpy as np
import concourse.bass as bass
import concourse.mybir as mybir
from concourse.bass_utils import run_bass_kernel_spmd

F32 = mybir.dt.float32
BF16 = mybir.dt.bfloat16
AF = mybir.ActivationFunctionType
ALU = mybir.AluOpType
AX = mybir.AxisListType

EPOCH = 30000
NDMASEM = 12
NEG = -30000.0
EPS = 1e-6
NCORES = 8


def _dsize(dt):
    return mybir.dt.size(dt)


class Tok:
    __slots__ = ("key", "val", "clock", "fin")

    def __init__(self, key, val, clock):
        self.key, self.val, self.clock = key, val, clock
        self.fin = 0.0


def _nelem(ap):
    n = 1
    for (_s, c) in list(ap.ap)[1:]:
        n *= c
    return n


SYNC_LAT = 0.35
PE_WAIT_MARGIN = 2.0


class Sched:
    ENG = ("pe", "act", "dve", "pool", "sp")

    def __init__(self, nc):
        self.nc = nc
        self.prog = {e: [] for e in self.ENG}
        self.count = {e: 0 for e in self.ENG}
        self.knows = {e: {} for e in self.ENG}
        self.sems = {}
        self.dma_i = {e: 0 for e in self.ENG}
        self.dma_cnt = {}
        self.dma_last = {}
        self.bufs = {}
        self._ctx = []
        self.nwaits = 0
        self.nops = 0
        self.marks = []
        self.capture = None
        self.free = {e: 0.0 for e in self.ENG}

    def _cost(self, eng, writes, reads):
        ap = writes[0] if writes else (reads[0] if reads else None)
        n = _nelem(ap) if ap is not None else 1
        if reads:
            n = max(n, max(_nelem(r) for r in reads) if eng != "pe" else n)
        if eng == "dve":
            return 0.06 + n * 0.00125
        if eng == "act":
            return 0.2 + n * 0.0008
        if eng == "pool":
            return 0.1 + n * 0.002
        return 0.1

    def _est_start(self, eng, deps):
        t = self.free[eng]
        for d in deps:
            f = d.fin + (0.0 if d.key == ("e", eng) else SYNC_LAT)
            if f > t:
                t = f
        return t

    def run_streams(self, streams):
        ptr = [0] * len(streams)
        while True:
            best = None
            for i, stm in enumerate(streams):
                if ptr[i] >= len(stm):
                    continue
                kind, eng, a, b, c, cost = stm[ptr[i]]
                if kind == "op":
                    deps, _, _ = self._deps_for(b, c, eng)
                else:
                    deps, _, _ = self._deps_for([b], [a], "dma")
                st_ = self._est_start(eng, deps)
                if eng == "pe" and st_ > self.free[eng] + 1e-9:
                    st_ += PE_WAIT_MARGIN
                if best is None or (st_, i) < best[0]:
                    best = ((st_, i), i)
            if best is None:
                break
            i = best[1]
            kind, eng, a, b, c, cost = streams[i][ptr[i]]
            ptr[i] += 1
            if kind == "op":
                self.op(eng, a, b, c, cost=cost)
            else:
                self.dma(a, b, eng=eng, **c)

    def _sem(self, key):
        s = self.sems.get(key)
        if s is None:
            cm = self.nc.semaphore("s_" + "_".join(str(k) for k in key))
            s = cm.__enter__()
            self._ctx.append(cm)
            self.sems[key] = s
        return s

    def _eng_sem_val(self, eng, v):
        ep = (v - 1) // EPOCH
        return self._sem((eng, ep)), v - ep * EPOCH

    @staticmethod
    def _intervals(pairs, limit=64):
        pairs = [(abs(st), c) for (st, c) in pairs if c > 1]
        if not pairs:
            return [(0, 1)]
        pairs.sort(key=lambda p: -p[0])
        out = [(0, 0)]
        res = []

        def rec(base, idx, budget):
            rest = pairs[idx:]
            ext = sum((c - 1) * st for st, c in rest) + 1
            if not rest:
                return [(base, base + 1)]
            st0, c0 = rest[0]
            inner = sum((c - 1) * st for st, c in rest[1:]) + 1
            if st0 > inner and c0 <= budget:
                r = []
                for i in range(c0):
                    r += rec(base + i * st0, idx + 1, max(1, budget // c0))
                return r
            return [(base, base + ext)]
        return rec(0, 0, limit)

    @staticmethod
    def regions(ap):
        t = ap.tensor
        name = t.name
        esz = _dsize(ap.dtype)
        pairs = [tuple(p) for p in ap.ap]
        tn = type(t).__name__
        if not ("SB" in tn or "PSum" in tn):
            return [(name, 0, 1, (ap.offset + lo) * esz, (ap.offset + hi) * esz)
                    for (lo, hi) in Sched._intervals(pairs)]
        prow = 1
        for d in list(t.shape)[1:]:
            prow *= d
        prow_b = prow * _dsize(t.dtype)
        if "PSum" in tn:
            return [("~" + name, 0, 128, 0, prow_b)]
        off_b = ap.offset * esz
        p0 = off_b // prow_b
        lo0 = off_b - p0 * prow_b
        pstep, pcnt = pairs[0]
        if pcnt > 1:
            pst = (pstep * esz) // prow_b
            p1 = p0 + (pcnt - 1) * max(pst, 1) + 1
        else:
            p1 = p0 + 1
        return [(name, p0, p1, (lo0 + lo * esz) // 4 * 4, (lo0 + hi * esz + 3) // 4 * 4)
                for (lo, hi) in Sched._intervals(pairs[1:])]

    def _deps_for(self, reads, writes, eng=None):
        deps = []
        rr = [r for a in reads for r in self.regions(a)]
        wr = [r for a in writes for r in self.regions(a)]
        for (name, p0, p1, lo, hi) in rr:
            b = self.bufs.get(name)
            if b is None:
                continue
            if name[0] == "~":
                for (q0, q1, l2, h2, tok) in b["w"]:
                    e2, w2 = b["meta"]
                    if e2 == eng and not w2:
                        continue
                    deps.append(tok)
                continue
            for (q0, q1, l2, h2, tok) in b["w"]:
                if lo < h2 and l2 < hi and p0 < q1 and q0 < p1:
                    deps.append(tok)
        for (name, p0, p1, lo, hi) in wr:
            b = self.bufs.get(name)
            if b is None:
                continue
            if name[0] == "~":
                for (q0, q1, l2, h2, tok) in b["w"]:
                    e2, w2 = b["meta"]
                    if e2 == eng and eng == "pe":
                        continue
                    deps.append(tok)
                continue
            for (q0, q1, l2, h2, tok) in b["w"]:
                if lo < h2 and l2 < hi and p0 < q1 and q0 < p1:
                    deps.append(tok)
            for (q0, q1, l2, h2, tok) in b["r"]:
                if lo < h2 and l2 < hi and p0 < q1 and q0 < p1:
                    deps.append(tok)
        return deps, rr, wr

    def _record(self, rr, wr, tok, eng=None):
        for (name, p0, p1, lo, hi) in wr:
            b = self.bufs.setdefault(name, {"w": [], "r": []})
            if name[0] == "~":
                b["w"] = [(p0, p1, lo, hi, tok)]
                b["meta"] = (eng, True)
                continue
            b["w"] = [e for e in b["w"] if not (p0 <= e[0] and e[1] <= p1 and lo <= e[2] and e[3] <= hi)]
            b["r"] = [e for e in b["r"] if not (p0 <= e[0] and e[1] <= p1 and lo <= e[2] and e[3] <= hi)]
            b["w"].append((p0, p1, lo, hi, tok))
        for (name, p0, p1, lo, hi) in rr:
            b = self.bufs.setdefault(name, {"w": [], "r": []})
            if name[0] == "~":
                b["w"] = [(p0, p1, lo, hi, tok)]
                b["meta"] = (eng, False)
                continue
            b["r"] = [e for e in b["r"] if not (e[0] == p0 and e[1] == p1 and e[2] == lo and e[3] == hi
                                                and e[4].key == tok.key and e[4].val <= tok.val)]
            b["r"].append((p0, p1, lo, hi, tok))

    def _waits(self, eng, deps):
        kn = self.knows[eng]
        need = {}
        for t in deps:
            if kn.get(t.key, 0) >= t.val:
                continue
            cur = need.get(t.key)
            if cur is None or cur.val < t.val:
                need[t.key] = t
        out = []
        for t in sorted(need.values(), key=lambda t: -len(t.clock)):
            if kn.get(t.key, 0) >= t.val:
                continue
            out.append(t)
            for k, v in t.clock.items():
                if kn.get(k, 0) < v:
                    kn[k] = v
        return out

    def _emit_waits(self, eng, toks):
        for t in toks:
            self.nwaits += 1
            if t.key[0] == "d":
                sem, val = self._sem(t.key), t.val
            else:
                sem, val = self._eng_sem_val(t.key[1], t.val)
            self.prog[eng].append(lambda e, sem=sem, val=val: e.wait_ge(sem, val))

    def op(self, eng, fn, reads, writes, cost=None):
        if cost is None:
            cost = self._cost(eng, writes, reads)
        if self.capture is not None:
            self.capture.append(("op", eng, fn, reads, writes, cost))
            return None
        self.nops += 1
        deps, rr, wr = self._deps_for(reads, writes, eng)
        start = self._est_start(eng, deps)
        w = self._waits(eng, deps)
        self._emit_waits(eng, w)
        self.count[eng] += 1
        v = self.count[eng]
        sem, _lv = self._eng_sem_val(eng, v)
        key = ("e", eng)
        clock = dict(self.knows[eng])
        clock[key] = v
        tok = Tok(key, v, clock)
        tok.fin = start + cost
        self.free[eng] = start + cost
        self.prog[eng].append(lambda e, fn=fn, sem=sem: fn(e).then_inc(sem, 1))
        self._record(rr, wr, tok, eng)
        return tok

    def dma(self, out, in_, eng="sp", **kw):
        if self.capture is not None:
            self.capture.append(("dma", eng, out, in_, kw, 0.0))
            return None
        self.nops += 1
        deps, rr, wr = self._deps_for([in_], [out], "dma")
        start = self._est_start(eng, deps)
        nbytes = _nelem(out) * 128 * _dsize(out.dtype)
        self.free[eng] = start + (0.6 if eng == "pool" else 0.08)
        i = self.dma_i[eng]
        self.dma_i[eng] += 1
        skey = ("d", eng, i % NDMASEM)
        prev = self.dma_last.get(skey)
        if prev is not None:
            deps.append(prev)
        w = self._waits(eng, deps)
        self._emit_waits(eng, w)
        val = self.dma_cnt.get(skey, 0) + 16
        self.dma_cnt[skey] = val
        sem = self._sem(skey)
        clock = dict(self.knows[eng])
        clock[skey] = val
        tok = Tok(skey, val, clock)
        tok.fin = start + 2.0 + nbytes / 80e3
        self.dma_last[skey] = tok
        self.prog[eng].append(
            lambda e, out=out, in_=in_, sem=sem, kw=kw: e.dma_start(out=out, in_=in_, **kw).then_inc(sem, 16))
        self._record(rr, wr, tok, "dma")
        return tok

    def mm(self, out, pairs):
        n = len(pairs)
        reads = []
        for (l, r) in pairs:
            reads += [l, r]

        def fn(e):
            ins = None
            for i, (l, r) in enumerate(pairs):
                ins = e.matmul(out, l, r, start=(i == 0), stop=(i == n - 1))
            return ins
        f32 = 8.0 if _dsize(pairs[0][1].dtype) == 4 else 1.0
        cost = sum((max(_nelem(r), 128) / 2100.0 + 0.01) * f32 for (_l, r) in pairs)
        return self.op("pe", fn, reads, [out], cost=cost)

    def tr(self, out, in_, ident):
        return self.op("pe", lambda e: e.transpose(out, in_, ident), [in_, ident], [out], cost=0.14)

    def finish(self):
        w = self._waits("sp", list(self.dma_last.values()))
        self._emit_waits("sp", w)
        nc = self.nc
        prog = self.prog
        with nc.Block() as block:
            @block.tensor
            def _(e):
                for f in prog["pe"]:
                    f(e)

            @block.scalar
            def _(e):
                for f in prog["act"]:
                    f(e)

            @block.vector
            def _(e):
                for f in prog["dve"]:
                    f(e)

            @block.gpsimd
            def _(e):
                for f in prog["pool"]:
                    f(e)

            @block.sync
            def _(e):
                for f in prog["sp"]:
                    f(e)
        for cm in reversed(self._ctx):
            cm.__exit__(None, None, None)
        self._ctx = []


DM = 1024
INC = 3080
DFF = 4096
PLE = 256
NPT = 16
ARENA_F32 = 37376
CM_IDENT, CM_ONES, CM_CUM, CM_MASK, CM_MASKT, CM_SEL, CM_01 = 0, 1, 2, 4, 6, 8, 10
RV_LNG, RV_LNB, RV_BI, RV_BF, RV_MLG = 0, 512, 1024, 1028, 1032
RV_N = RV_MLG
GN_MIX, GN_FFN, GN_PLE, GN_BS = 0, 8, 16, 24


def build():
    nc = bass.Bass("TRN2", target_bir_lowering=False)

    def din(name, shape):
        return nc.dram_tensor(name, shape, F32, kind="ExternalInput").ap()

    def dout(name, shape):
        return nc.dram_tensor(name, shape, F32, kind="ExternalOutput").ap()

    xp = din("xp", [2048, DM]); xs = din("xs", [128, DM])
    pp = din("pp", [2048, PLE]); psm = din("psm", [128, PLE])
    sC = din("sC", [16, 4, 128, 128]); snT = din("snT", [4, 128, 16]); smt = din("smt", [128, 4])
    w_in = din("w_in", [DM, INC]); w_out = din("w_out", [DM, DM])
    w_up = din("w_up", [DM, DFF]); w_down = din("w_down", [DFF, DM])
    w_pg = din("w_pg", [DM, DM]); w_pp = din("w_pp", [PLE, DM])
    gains = din("gains", [128, 32]); rowv = din("rowv", [RV_N]); nfin = din("nfin", [DM]); mlg = din("mlg", [512])
    wsg = din("wsg", [8, 128, 128])
    cmat = din("cmat", [12, 128, 128]); csm = din("csm", [128, 32])

    y_p = dout("y_p", [2048, DM]); y_s = dout("y_s", [128, DM])
    o_Cp = dout("o_Cp", [4, 128, 128]); o_Np = dout("o_Np", [4, 128]); o_Mp = dout("o_Mp", [1, 4])
    o_Vp = dout("o_Vp", [128, 512])
    o_Cs = dout("o_Cs", [16, 4, 128, 128]); o_NsT = dout("o_NsT", [4, 128, 16]); o_Ms = dout("o_Ms", [16, 4])
    o_Vs = dout("o_Vs", [128, 512])

    S = Sched(nc)
    from contextlib import ExitStack
    with ExitStack() as st:
        def sb(name, shape, dt=F32):
            return st.enter_context(nc.sbuf_tensor(name, shape, dt))

        def aps(*xs_):
            return [x for x in xs_ if not isinstance(x, (int, float)) and x is not None]

        def act(out, in_, func, bias=None, scale=None, accum=None):
            kw = {}
            if bias is not None:
                kw["bias"] = bias
            if scale is not None:
                kw["scale"] = scale
            if accum is not None:
                kw["accum_out"] = accum
            return S.op("act", lambda e: e.activation(out=out, in_=in_, func=func, **kw),
                        aps(in_, bias, scale), aps(out, accum))

        def tt(eng, out, in0, in1, op):
            return S.op(eng, lambda e: e.tensor_tensor(out=out, in0=in0, in1=in1, op=op), [in0, in1], [out])

        def ts(eng, out, in0, s1, op0, s2=None, op1=None, accum=None):
            kw = {}
            if op1 is not None:
                kw["op1"] = op1
            if accum is not None:
                kw["accum_out"] = accum
            return S.op(eng, lambda e: e.tensor_scalar(out=out, in0=in0, scalar1=s1, scalar2=s2, op0=op0, **kw),
                        aps(in0, s1, s2), aps(out, accum))

        def stt(eng, out, in0, scalar, in1, op0, op1, accum=None):
            kw = {}
            if accum is not None:
                kw["accum_out"] = accum
            return S.op(eng, lambda e: e.scalar_tensor_tensor(out=out, in0=in0, scalar=scalar, in1=in1,
                                                              op0=op0, op1=op1, **kw),
                        aps(in0, scalar, in1), aps(out, accum))

        def cp(eng, out, in_):
            if eng == "act":
                return act(out, in_, AF.Copy)
            return S.op(eng, lambda e: e.tensor_copy(out=out, in_=in_), [in_], [out])

        def bnstats(out, in_):
            return S.op("dve", lambda e: e.bn_stats(out=out, in_=in_), [in_], [out])

        def bnaggr(out, in_):
            return S.op("dve", lambda e: e.bn_aggr(out=out, in_=in_), [in_], [out])

        def redmax(out, in_):
            return S.op("dve", lambda e: e.tensor_reduce(out=out, in_=in_, axis=AX.X, op=ALU.max), [in_], [out])

        def recip(out, in_):
            return S.op("dve", lambda e: e.reciprocal(out=out, in_=in_), [in_], [out])

        def memset(eng, out, val):
            return S.op(eng, lambda e: e.memset(out, val), [], [out])

        CM = sb("CM", [128, 12, 128])
        CS = sb("CS", [128, 32])
        GN = sb("GN", [128, 32])
        MLGH = sb("MLGH", [128, 512])
        RV = sb("RV", [128, RV_N])
        IDB = sb("IDB", [128, 128], BF16)
        ONESB = sb("ONESB", [128, 128], BF16)
        SMALLB = sb("SMALLB", [128, 5, 32], BF16)
        WM = sb("WM", [128, 8, 128], BF16)
        NEGH = sb("NEGH", [128, 8])
        SMT = sb("SMT", [128, 4])
        MRUN = sb("MRUN", [128, 4])
        CST = sb("CST", [128, 4, 129])
        CSTB = sb("CSTB", [128, 4, 129], BF16)
        KTOKS = sb("KTOKS", [128, 4, 128], BF16)
        VWS = sb("VWS", [128, 4, 130], BF16)
        DECB = sb("DECBS", [128, 16, 4])
        NCOL = sb("NCOLS", [128, 4, 16])
        H = sb("H", [128, 9, DM])
        SMALL = sb("SMALL", [128, 5, 160])
        AR = sb("AR", [128, ARENA_F32])
        ARB = AR.bitcast(BF16)
        PB = [st.enter_context(nc.psum_tensor(f"pb{i}", [128, 512], F32)) for i in range(8)]
        PBB = [p.bitcast(BF16) for p in PB]

        class Arena:
            def __init__(self, off=0):
                self.off = off

            def f32(self, n, shape=None):
                assert self.off % 4 == 0
                o = self.off // 4
                self.off += n * 4
                assert self.off <= ARENA_F32 * 4, ("arena overflow", self.off)
                v = AR[:, o:o + n]
                return v if shape is None else v.rearrange(shape[0], **shape[1])

            def bf(self, n, shape=None):
                o = self.off // 2
                self.off += ((n * 2 + 3) // 4) * 4
                assert self.off <= ARENA_F32 * 4, ("arena overflow", self.off)
                v = ARB[:, o:o + n]
                return v if shape is None else v.rearrange(shape[0], **shape[1])

        class Banks:
            def __init__(self, ids):
                self.ids = list(ids)
                self.i = 0

            def next(self):
                b = self.ids[self.i % len(self.ids)]
                self.i += 1
                return b

        S.dma(H[:, 0, :], xp[0:128, :])
        S.dma(CM[:].rearrange("p k n -> p (k n)").rearrange("p (k n) -> p k n", k=12), cmat.rearrange("k p n -> p k n"))
        S.dma(CS[:], csm)
        S.dma(GN[:], gains)
        S.dma(RV[:], rowv.partition_broadcast(128))
        S.dma(SMT[:], smt)
        memset("pool", NEGH[:], -0.5)
        S.dma(MLGH[:], mlg.partition_broadcast(128))
        ts("dve", MLGH[:], MLGH[:], 0.5, ALU.mult)
        cp("dve", IDB[:], CM[:, CM_IDENT, :])
        cp("dve", ONESB[:], CM[:, CM_ONES, :])
        IDF = CM[:, CM_IDENT, :]
        ONESF = CM[:, CM_ONES, :]
        memset("pool", MRUN[:], 0.0)
        memset("pool", CST[:], 0.0)
        memset("pool", CSTB[:], 0.0)
        A0 = Arena(64 * 1024)
        wtmp = A0.f32(8 * 128, ("p (k n) -> p k n", dict(k=8)))
        wtb = A0.bf(8 * 128, ("p (k n) -> p k n", dict(k=8)))
        S.dma(wtmp, wsg.rearrange("k p n -> p k n"))
        for ty in range(2):
            tt("dve", wtb[:, ty * 4:(ty + 1) * 4, :], wtmp[:, ty * 4:(ty + 1) * 4, :],
               CM[:, CM_01 + ty, :].unsqueeze(1).to_broadcast([128, 4, 128]), ALU.mult)
        for k in range(8):
            S.tr(PBB[0][:, k * 128:(k + 1) * 128], wtb[:, k, :], IDB[:])
        cp("dve", WM[:], PBB[0][:, 0:1024].rearrange("p (k n) -> p k n", k=8))

        def load_cast(dst_bf, src_dram):
            S.dma(dst_bf, src_dram, eng="pool")

        def small(d, off, n):
            return SMALL[:, d, off:off + n]

        def bcast_rows(d, soff, vec, dg, ps_out):
            hl = SMALLB[:, d, soff:soff + 8]
            cp("dve", hl[:, 0:4], vec)
            tt("dve", hl[:, 4:8], vec, hl[:, 0:4], ALU.subtract)
            tt("pool", dg, IDB[:].unsqueeze(1).to_broadcast([128, 8, 128]),
               hl.unsqueeze(2).to_broadcast([128, 8, 128]), ALU.mult)
            S.mm(ps_out, [(ONESB[:], dg[:, 0:4, :].rearrange("p h s -> p (h s)")),
                          (ONESB[:], dg[:, 4:8, :].rearrange("p h s -> p (h s)"))])

        def rstd_from_ss(tmp, ss, n, inv_d, out):
            ts("pool", tmp, ss, inv_d, ALU.mult, EPS, ALU.add)
            tt("pool", out, tmp, NEGH[:, 0:n], ALU.pow)

        def norm_T(d, hsrc, a_bf, aT_dst, ssoff, goff, bank, junk=None):
            ss = small(d, ssoff, 1)
            rs = small(d, ssoff + 1, 1)
            tmp = small(d, ssoff + 2, 1)
            memset("pool", ss, 0.0)
            act(a_bf if junk is None else junk, hsrc, AF.Square, accum=ss)
            rstd_from_ss(tmp, ss, 1, 1.0 / DM, rs)
            act(a_bf, hsrc, AF.Copy, scale=rs)
            for kc in range(8):
                S.tr(PBB[bank][:, kc * 128:(kc + 1) * 128], a_bf[:, kc * 128:(kc + 1) * 128], IDB[:])
            tt("dve", aT_dst, PBB[bank][:, 0:1024].rearrange("p (k t) -> p k t", k=8),
               GN[:, goff:goff + 8].unsqueeze(2).to_broadcast([128, 8, 128]), ALU.mult)

        def run_pipeline(tiles, stages, background=None):
            n = len(tiles)
            for it in range(n + len(stages) - 1):
                streams = []
                for s_i in reversed(range(len(stages))):
                    ti = it - s_i
                    if 0 <= ti < n:
                        S.capture = []
                        for _ in stages[s_i](tiles[ti]):
                            pass
                        streams.append(S.capture)
                        S.capture = None
                busy = {}
                for stm in streams:
                    for o in stm:
                        busy[o[1]] = busy.get(o[1], 0.0) + o[5]
                if background is not None and it in background and background[it] is not None:
                    streams.append(background[it])
                S.run_streams(streams)
                S.marks.append(("it", it, dict(S.free), busy))

        def ptile(i):
            return dict(ty=0, i=i, x=xp[i * 128:(i + 1) * 128, :], p=pp[i * 128:(i + 1) * 128, :],
                        y=y_p[i * 128:(i + 1) * 128, :])
        stile = dict(ty=1, i=0, x=xs, p=psm, y=y_s)
        SGS = [[ptile(i) for i in range(4)] + [stile] + [ptile(i) for i in range(4, 8)],
               [ptile(i) for i in range(8, 16)]]
        gidx = [0]

        R = Arena()
        R1_OFF = R.off
        WG = R.bf(8 * DM, ("p (k n) -> p k n", dict(k=8)))
        WP = R.bf(2 * DM, ("p (k n) -> p k n", dict(k=2)))
        NF = R.f32(DM)
        R2_OFF = R.off

        WA_OFF = ARENA_F32 * 4 - (8 * INC + 8 * DM) * 2
        WA = Arena(WA_OFF)
        WIN = WA.bf(8 * INC, ("p (k n) -> p k n", dict(k=8)))
        WOUT = WA.bf(8 * DM, ("p (k n) -> p k n", dict(k=8)))
        COLG = ((0, 512), (512, 512), (2560, 512), (1024, 512), (1536, 512), (2048, 512), (3072, 8))

        BWA = Arena(WA_OFF)
        WU = [BWA.bf(8 * 1024, ("p (k n) -> p k n", dict(k=8))), None]
        WD = [BWA.bf(8 * 1024, ("p (k n) -> p k n", dict(k=8))), None]
        WU[1] = BWA.bf(8 * 1024, ("p (k n) -> p k n", dict(k=8)))
        assert BWA.off <= WA_OFF + 8 * INC * 2
        WD[1] = Arena(R2_OFF + (8 * 9 * 128 + DM + 2 * 8 * 512) * 2).bf(8 * 1024, ("p (k n) -> p k n", dict(k=8)))

        def load_q(qd):
            wu, wd = WU[qd % 2], WD[qd % 2]
            w_up_v = w_up.rearrange("(k p) n -> p k n", p=128)
            for hf in range(2):
                c0 = qd * 1024 + hf * 512
                load_cast(wu[:, :, hf * 512:(hf + 1) * 512], w_up_v[:, :, c0:c0 + 512])
            for hf in range(2):
                r0 = qd * 1024 + hf * 512
                load_cast(wd[:, hf * 4:(hf + 1) * 4, :],
                          w_down[r0:r0 + 512, :].rearrange("(f p) n -> p f n", p=128))

        def load_phaseA_weights(by_group):
            S.capture = []
            w_in_v = w_in.rearrange("(k p) n -> p k n", p=128)
            for gi_, (c0, n) in enumerate(COLG):
                load_cast(WIN[:, :, c0:c0 + n], w_in_v[:, :, c0:c0 + n])
                if by_group and gi_ == 2:
                    for slot_i in range(1, 9):
                        S.dma(H[:, slot_i, :], SGS[0][slot_i]["x"], eng="pool")
            load_cast(WOUT, w_out.rearrange("(k p) n -> p k n", p=128))
            stream = S.capture
            S.capture = None
            return stream

        for sg_i, sg in enumerate(SGS):
            for slot_i, t in enumerate(sg):
                t["slot"] = slot_i
                t["g"] = gidx[0]
                gidx[0] += 1
                if sg_i > 0:
                    S.dma(H[:, slot_i, :], t["x"])

            A = Arena(0)
            AT = [A.bf(8 * 128, ("p (k n) -> p k n", dict(k=8))) for _ in range(2)]
            A_BF = A.bf(DM)
            U = [A.bf(512) for _ in range(2)]
            VN = [A.f32(512) for _ in range(2)]
            VNB = A.bf(512)
            TH = A.f32(512)
            QTOK = [A.bf(512) for _ in range(2)]
            YGM = A.bf(512)
            DGA = A.bf(1024, ("p (h s) -> p h s", dict(h=8)))
            PBIG = A.f32(512, ("p (h s) -> p h s", dict(h=4)))
            QT = [A.bf(512, ("p (h t) -> p h t", dict(h=4))) for _ in range(2)]
            KT = [A.bf(512, ("p (h t) -> p h t", dict(h=4))) for _ in range(2)]
            KTOK = [A.bf(512, ("p (h d) -> p h d", dict(h=4))) for _ in range(3)]
            VAUG = [A.bf(4 * 130, ("p (h e) -> p h e", dict(h=4)))[:, :, 0:129] for _ in range(3)]
            SGM = [A.bf(512, ("p (h e) -> p h e", dict(h=4))) for _ in range(4)]
            YT = [A.bf(8 * 128, ("p (k t) -> p k t", dict(k=8))) for _ in range(3)]
            NUM = [A.f32(4 * 129, ("p (h e) -> p h e", dict(h=4))) for _ in range(2)]
            DGC = A.bf(1024, ("p (h s) -> p h s", dict(h=8)))
            BIG1 = A.f32(512, ("p (h s) -> p h s", dict(h=4)))
            STB = A.bf(512, ("p (h s) -> p h s", dict(h=4)))
            QCS = A.f32(4 * 129, ("p (h e) -> p h e", dict(h=4)))
            VW = A.bf(4 * 130, ("p (h e) -> p h e", dict(h=4)))[:, :, 0:129]
            YML = A.bf(512)
            JUNK = A.f32(128)
            has_s = any(t["ty"] == 1 for t in sg)
            if has_s:
                CSB3 = [A.bf(16 * 130, ("p (j e) -> p j e", dict(j=16)))[:, :, 0:129] for _ in range(3)]
                QTD = A.bf(16 * 128, ("p (j t) -> p j t", dict(j=16)))
                DECS = A.f32(64, ("p (j h) -> p j h", dict(j=16)))
            assert A.off <= WA_OFF, ("phase A work buffers overflow into weights", A.off, WA_OFF)
            bgA = load_phaseA_weights(True) if sg_i == 0 else None
            if has_s:
                memset("pool", QTD, 0.0)
            for v_ in VAUG:
                memset("pool", v_[:, :, 128:129], 1.0)

            bP = Banks([1, 2, 3])
            bP1 = Banks([4])
            bM1 = Banks([5, 6])
            bM2 = Banks([7])

            def hv(bx, by, h):
                b_ = PB[bx] if h % 2 == 0 else PB[by]
                return b_[:, (h // 2) * 129:(h // 2) * 129 + 129]

            def stage_N(t):
                g = t["g"]
                norm_T(g % 5, H[:, t["slot"], :], A_BF, AT[g % 2], 0, GN_MIX, 0)
                yield

            def stage_P(t):
                ty = t["ty"]; g = t["g"]; d2 = g % 2; d3 = g % 3; d4 = g % 4
                sm = lambda off, n: small(g % 5, off, n)
                at = AT[d2]
                lhs = [at[:, kc, :] for kc in range(8)]

                def proj(col0, n, bank):
                    ps = PB[bank][:, 0:n]
                    S.mm(ps, [(lhs[kc], WIN[:, kc, col0:col0 + n]) for kc in range(8)])
                    return ps
                b_u, b_v, b_o = bP.next(), bP.next(), bP.next()
                ps_u = proj(0, 512, b_u)
                ps_v = proj(512, 512, b_v)
                ps_o = proj(2560, 512, b_o)
                act(U[d2], ps_u, AF.Gelu_apprx_tanh)
                act(VN[d2], ps_v, AF.Gelu_apprx_tanh)
                act(TH, ps_o, AF.Tanh, scale=0.5)
                yield
                stt("dve", SGM[d4], TH.rearrange("p (h e) -> p h e", h=4), 1.0,
                    MLGH[:].rearrange("p (h e) -> p h e", h=4), ALU.add, ALU.mult)
                ps_q = proj(1024, 512, bP.next())
                cp("act", QTOK[d2], ps_q)
                yield
                ps_k = proj(1536, 512, bP.next())
                act(KTOK[d3], ps_k.rearrange("p (h d) -> p h d", h=4), AF.Copy, scale=float(128 ** -0.5))
                yield
                ps_vm = proj(2048, 512, bP.next())
                cp("dve", VAUG[d3][:, :, 0:128], ps_vm.rearrange("p (h d) -> p h d", h=4))
                yield
                bg = bP.next()
                ps_g = proj(3072, 8, bg)
                zg = sm(4, 8)
                cp("dve", zg, ps_g)
                yield

            def stage_P1(t):
                ty = t["ty"]; g = t["g"]; d2 = g % 2; d3 = g % 3; d4 = g % 4
                sm = lambda off, n: small(g % 5, off, n)
                zg = sm(4, 8)
                bP = bP1
                U_, VN_ = U[d2], VN[d2]
                bt = bP.next()
                for h in range(4):
                    S.tr(PBB[bt][:, h * 128:(h + 1) * 128], QTOK[d2][:, h * 128:(h + 1) * 128], IDB[:])
                for h in range(4):
                    S.tr(PBB[bt][:, 512 + h * 128:512 + (h + 1) * 128], KTOK[d3][:, h, :], IDB[:])
                cp("act", QT[d2], PBB[bt][:, 0:512].rearrange("p (h t) -> p h t", h=4))
                cp("dve", KT[d2], PBB[bt][:, 512:1024].rearrange("p (h t) -> p h t", h=4))
                yield
                st6 = sm(12, 6); mv = sm(18, 2); lrs = sm(20, 1); ltmp = sm(21, 1)
                bnstats(st6, VN_)
                bnaggr(mv, st6)
                rstd_from_ss(ltmp, mv[:, 1:2], 1, 1.0, lrs)
                ts("dve", VN_, VN_, mv[:, 0:1], ALU.subtract, lrs, ALU.mult)
                yield
                tt("pool", VN_, VN_, RV[:, RV_LNG:RV_LNG + 512], ALU.mult)
                tt("pool", VN_, VN_, RV[:, RV_LNB:RV_LNB + 512], ALU.add)
                cp("pool", VNB, VN_)
                if ty == 1:
                    S.dma(o_Vs, VN_)
                elif t["i"] == NPT - 1:
                    S.dma(o_Vp, VN_)
                yield
                bs_ = bP.next()
                ps_s = PB[bs_][:, :]
                for h in range(4):
                    S.mm(ps_s[:, h * 128:(h + 1) * 128], [(WM[:, ty * 4 + h, :], VNB[:, h * 128:(h + 1) * 128])])
                for h in range(4):
                    stt("dve", YGM[:, h * 128:(h + 1) * 128], ps_s[:, h * 128:(h + 1) * 128],
                        GN[:, GN_BS + ty * 4 + h:GN_BS + ty * 4 + h + 1], U_[:, h * 128:(h + 1) * 128], ALU.add, ALU.mult)
                yield
                by = bP.next()
                for h in range(4):
                    S.tr(PBB[by][:, h * 128:(h + 1) * 128], YGM[:, h * 128:(h + 1) * 128], IDB[:])
                cp("act", YT[d3][:, 0:4, :], PBB[by][:, 0:512].rearrange("p (k t) -> p k t", k=4))
                yield
                ig = sm(24, 4); e1 = sm(28, 4); l1 = sm(32, 4); bb = sm(36, 4); aa = sm(44, 4)
                mx = sm(48, 4); xf = sm(120, 4)
                tt("dve", ig, zg[:, 0:4], RV[:, RV_BI:RV_BI + 4], ALU.add)
                tt("dve", xf, zg[:, 4:8], RV[:, RV_BF:RV_BF + 4], ALU.add)
                act(e1, xf, AF.Exp, scale=-1.0)
                act(l1, e1, AF.Ln, bias=1.0)
                bc = bP.next()
                ps_c = PB[bc][:, 0:4]
                S.mm(ps_c, [(CM[:, CM_CUM + ty, :], l1)])
                ts("dve", bb, ps_c, -1.0, ALU.mult)
                tt("dve", aa, ig, bb, ALU.subtract)
                yield
                ba = bP.next()
                ps_A = PB[ba][:, :]
                bcast_rows(g % 5, 0, aa, DGA, ps_A)
                tt("dve", PBIG, ps_A.rearrange("p (h s) -> p h s", h=4),
                   CM[:, CM_MASK + ty, :].unsqueeze(1).to_broadcast([128, 4, 128]), ALU.add)
                redmax(mx, PBIG)
                tt("dve", mx, mx, bb, ALU.add)
                if ty == 1:
                    S.dma(NCOL[:], snT.rearrange("h d j -> d h j"))
                    for h in range(3):
                        S.dma(CSB3[h][:, :, 0:128], sC[:, h, :, :].rearrange("j d e -> d j e"), eng="pool")
                yield

            def stage_M1(t):
                ty = t["ty"]; g = t["g"]; d2 = g % 2; d3 = g % 3
                sm = lambda off, n: small(g % 5, off, n)
                bb = sm(36, 4); gg = sm(40, 4); aa = sm(44, 4); mx = sm(48, 4); mt = sm(52, 4); cc = sm(56, 4)
                wint = sm(60, 4); negm = sm(64, 4); emt = sm(68, 4); mb8 = sm(72, 8); lsel = sm(80, 8)
                wend = sm(88, 4); dec = sm(92, 4); tmp4 = sm(124, 4); tmp5 = sm(128, 4)
                mprev = MRUN[:] if ty == 0 else SMT[:]
                tt("dve", gg, bb, mprev, ALU.add)
                tt("dve", mt, mx, gg, ALU.max)
                tt("dve", cc, bb, mt, ALU.subtract)
                cp("dve", mb8[:, 0:4], mt)
                cp("dve", mb8[:, 4:8], bb)
                bl = bM1.next()
                ps_l = PB[bl][:, 0:8]
                S.mm(ps_l, [(CM[:, CM_SEL + ty, :], mb8)])
                cp("dve", lsel, ps_l)
                if ty == 0:
                    yield
                ts("dve", negm, mt, -1.0, ALU.mult)
                tt("dve", tmp4, gg, mt, ALU.subtract)
                act(wint, tmp4, AF.Exp)
                act(emt, negm, AF.Exp)
                yield
                bC = bM1.next()
                ps_C = PB[bC][:, :]
                bcast_rows(g % 5, 8, cc, DGC, ps_C)
                tt("dve", BIG1, ps_C.rearrange("p (h s) -> p h s", h=4),
                   CM[:, CM_MASKT + ty, :].unsqueeze(1).to_broadcast([128, 4, 128]), ALU.add)
                yield
                for h in range(4):
                    act(BIG1[:, h, :], BIG1[:, h, :], AF.Exp, bias=aa[:, h:h + 1])
                bsc = bM1.next()
                ps_sc = PB[bsc][:, :].rearrange("p (h t) -> p h t", h=4)
                for h in range(4):
                    S.mm(ps_sc[:, h, :], [(KT[d2][:, h, :], QT[d2][:, h, :])])
                tt("dve", STB, ps_sc, BIG1, ALU.mult)
                yield
                bx, by = bM1.next(), bM1.next()
                for h in range(4):
                    S.mm(hv(bx, by, h), [(STB[:, h, :], VAUG[d3][:, h, :])])
                tt("dve", tmp4, aa, lsel[:, 4:8], ALU.add)
                tt("dve", tmp4, tmp4, lsel[:, 0:4], ALU.subtract)
                act(wend, tmp4, AF.Exp)
                tt("dve", tmp5, lsel[:, 4:8], mprev, ALU.add)
                tt("dve", tmp5, tmp5, lsel[:, 0:4], ALU.subtract)
                act(dec, tmp5, AF.Exp)
                vw_t = VW if ty == 0 else VWS[:, :, 0:129]
                tt("pool", vw_t, VAUG[d3], wend.unsqueeze(2).to_broadcast([128, 4, 129]), ALU.mult)
                if ty == 1:
                    cp("pool", KTOKS[:], KTOK[d3])
                yield
                num = NUM[d2]
                if ty == 0:
                    cx, cy = bM1.next(), bM1.next()
                    for half, bk in enumerate((bx, by)):
                        cp("dve", num.rearrange("p (a b) e -> p a b e", b=2)[:, :, half, :],
                           PB[bk][:, 0:258].rearrange("p (h e) -> p h e", h=2))
                    for h in range(4):
                        S.mm(hv(cx, cy, h), [(QT[d2][:, h, :], CSTB[:, h, :])])
                    for h in range(4):
                        act(QCS[:, h, :], hv(cx, cy, h), AF.Copy, scale=wint[:, h:h + 1])
                    tt("dve", num, num, QCS, ALU.add)
                    yield
                    ux, uy = bM1.next(), bM1.next()
                    for h in range(4):
                        S.mm(hv(ux, uy, h), [(KTOK[d3][:, h, :], VW[:, h, :])])
                    for h in range(4):
                        stt("dve", CST[:, h, :], CST[:, h, :], dec[:, h:h + 1], hv(ux, uy, h), ALU.mult, ALU.add)
                    cp("pool", CSTB[:], CST[:])
                    cp("dve", MRUN[:], lsel[:, 0:4])
                    if t["i"] == NPT - 1:
                        for h in range(4):
                            S.dma(o_Cp[h], CST[:, h, 0:128])
                        S.dma(o_Np.rearrange("h d -> d h"), CST[:, :, 128], allow_slow_non_contiguous=True)
                        S.dma(o_Mp, lsel[0:1, 0:4])
                    yield
                else:
                    for half, bk in enumerate((bx, by)):
                        cp("dve", num.rearrange("p (a b) e -> p a b e", b=2)[:, :, half, :],
                           PB[bk][:, 0:258].rearrange("p (h e) -> p h e", h=2))
                    tt("dve", DECS, dec.unsqueeze(1).to_broadcast([128, 16, 4]),
                       CS[:, 0:16].unsqueeze(2).to_broadcast([128, 16, 4]), ALU.mult)
                    bd = bM1.next()
                    ps_d = PB[bd][:, 0:64]
                    S.mm(ps_d, [(ONESF, DECS.rearrange("p j h -> p (j h)"))])
                    cp("dve", DECB[:].rearrange("p j h -> p (j h)"), ps_d)
                    pst = list(mt.ap[0])[0]
                    S.dma(o_Ms, bass.AP(mt.tensor, mt.offset + 7 * pst, [[8 * pst, 16], [1, 4]]))
                    yield
                    for h in range(4):
                        CSB = CSB3[h % 3]
                        if h == 3:
                            S.dma(CSB[:, :, 0:128], sC[:, h, :, :].rearrange("j d e -> d j e"), eng="pool")
                        cp("dve", CSB[:, :, 128], NCOL[:, h, :])
                        dst = bass.AP(QTD.tensor, QTD.offset, [list(QTD.ap[0]), [128 + 8, 16], [1, 8]])
                        cp("dve", dst, QT[d2][:, h, :].rearrange("p (j t) -> p j t", j=16))
                        bq = bM1.next()
                        pq = PB[bq][:, 0:129]
                        S.mm(pq, [(QTD[:, j, :], CSB[:, j, :]) for j in range(16)])
                        act(QCS[:, h, :], pq, AF.Copy, scale=wint[:, h:h + 1])
                        tt("dve", num[:, h, :], num[:, h, :], QCS[:, h, :], ALU.add)
                        yield

            def stage_M2(t):
                g = t["g"]; d2 = g % 2; d3 = g % 3
                sm = lambda off, n: small(g % 5, off, n)
                emt = sm(68, 4); aden = sm(96, 4); rden = sm(100, 4)
                ssq = sm(104, 4); r2 = sm(108, 4); rs2 = sm(112, 4); scl = sm(116, 4); rtmp = sm(132, 4)
                num = NUM[d2]
                den = num[:, :, 128]
                stt("dve", aden, den, -1.0, den, ALU.mult, ALU.max)
                tt("dve", aden, aden, emt, ALU.max)
                recip(rden, aden)
                memset("pool", ssq, 0.0)
                for h in range(4):
                    act(JUNK, num[:, h, 0:128], AF.Square, accum=ssq[:, h:h + 1])
                yield
                tt("dve", r2, rden, rden, ALU.mult)
                tt("dve", r2, r2, ssq, ALU.mult)
                rstd_from_ss(rtmp, r2, 4, 1.0 / 128, rs2)
                tt("dve", scl, rden, rs2, ALU.mult)
                yield
                for h in range(4):
                    stt("dve", YML[:, h * 128:(h + 1) * 128], num[:, h, 0:128], scl[:, h:h + 1], SGM[g % 4][:, h, :],
                        ALU.mult, ALU.mult)
                yield
                b7 = bM2.next()
                for h in range(4):
                    S.tr(PBB[b7][:, h * 128:(h + 1) * 128], YML[:, h * 128:(h + 1) * 128], IDB[:])
                cp("act", YT[d3][:, 4:8, :], PBB[b7][:, 0:512].rearrange("p (k t) -> p k t", k=4))
                yield
                for n in range(2):
                    bh = bM2.next()
                    ps_h = PB[bh][:, :]
                    S.mm(ps_h, [(YT[d3][:, c, :], WOUT[:, c, n * 512:(n + 1) * 512]) for c in range(8)])
                    hs = H[:, t["slot"], n * 512:(n + 1) * 512]
                    tt("dve", hs, hs, ps_h, ALU.add)
                    yield

            S.marks.append(("A start", sg_i, max(S.free.values())))
            S.capture = []
            load_q(0)
            bgQ0 = S.capture
            S.capture = None
            run_pipeline(sg, [stage_N, stage_P, stage_P1, stage_M1, stage_M2],
                         background={0: bgA, len(sg) + 1: bgQ0})
            S.marks.append(("A end", sg_i, max(S.free.values())))

            B = Arena(R2_OFF)
            nt = len(sg)
            NTOK = nt * 128
            A2T = B.bf(8 * 9 * 128, ("p (k n) -> p k n", dict(k=8)))
            ABF = B.bf(DM)
            GT = [B.bf(8 * 512, ("p (k n) -> p k n", dict(k=8))) for _ in range(2)]
            wd1_chk = B.bf(8 * 1024)
            assert wd1_chk.offset == WD[1].offset, (wd1_chk.offset, WD[1].offset)
            RL = [B.f32(512) for _ in range(2)]
            assert B.off <= WA_OFF, ("phase B low region overflows", B.off, WA_OFF)
            B = Arena(WA_OFF + 8 * INC * 2)
            if has_s:
                CSF2 = [B.f32(8 * 129, ("p (j e) -> p j e", dict(j=8))) for _ in range(3)]
                KMK2 = [B.bf(8 * 128, ("p (j d) -> p j d", dict(j=8)))]
                NOUT = B.f32(64, ("p (h j) -> p h j", dict(h=4)))

            def sample_state_update():
                bU = Banks([6, 7])

                def load_chunk(c):
                    h_, j0 = c // 2, (c % 2) * 8
                    S.dma(CSF2[c % 3][:, :, 0:128], sC[j0:j0 + 8, h_, :, :].rearrange("j d e -> d j e"))
                load_chunk(0)
                load_chunk(1)
                load_chunk(2)
                for h in range(4):
                    for half in range(2):
                        c = h * 2 + half
                        j0 = half * 8
                        csf = CSF2[c % 3]
                        kmk = KMK2[0]
                        cp("dve", csf[:, :, 128], NCOL[:, h, j0:j0 + 8])
                        tt("pool", kmk, KTOKS[:, h, :].unsqueeze(1).to_broadcast([128, 8, 128]),
                           CS[:, 16 + j0:24 + j0].unsqueeze(2).to_broadcast([128, 8, 128]), ALU.mult)
                        for grp3 in ((0, 1, 2), (3, 4, 5), (6, 7)):
                            bj = bU.next()
                            for k3, jj in enumerate(grp3):
                                S.mm(PB[bj][:, k3 * 129:(k3 + 1) * 129], [(kmk[:, jj, :], VWS[:, h, 0:129])])
                            for k3, jj in enumerate(grp3):
                                stt("dve", csf[:, jj, :], csf[:, jj, :], DECB[:, j0 + jj, h:h + 1],
                                    PB[bj][:, k3 * 129:(k3 + 1) * 129], ALU.mult, ALU.add)
                        S.dma(o_Cs[j0:j0 + 8, h, :, :].rearrange("j d e -> d j e"), csf[:, :, 0:128])
                        cp("dve", NOUT[:, h, j0:j0 + 8], csf[:, :, 128])
                        if c + 3 < 8:
                            load_chunk(c + 3)
                S.dma(o_NsT.rearrange("h d j -> d h j"), NOUT)

            S.capture = []

            def norm_b(t, junk=None):
                norm_T(t["slot"] % 5, H[:, t["slot"], :], ABF,
                       A2T[:, :, t["slot"] * 128:(t["slot"] + 1) * 128], 136, GN_FFN, 6 + t["slot"] % 2, junk=junk)

            S.capture = None
            GSZ = 384 if NTOK % 384 == 0 else 512
            JNK = ARB[:, 2 * RL[0].offset:2 * RL[0].offset + DM]
            for t in sg[0:GSZ // 128]:
                norm_b(t, junk=JNK)
            late_norm = list(sg[GSZ // 128:])
            S.capture = []
            load_cast(WG, w_pg.rearrange("(k p) n -> p k n", p=128))
            load_cast(WP, w_pp.rearrange("(k p) n -> p k n", p=128))
            S.dma(NF, nfin.partition_broadcast(128))
            tgroups = [(g0, min(g0 + GSZ, NTOK)) for g0 in range(0, NTOK, GSZ)]
            gi_box = [0]

            def emit_up(qd, g0, g1):
                wu = WU[qd % 2]
                n = g1 - g0
                gt = GT[gi_box[0] % 2]
                gi_box[0] += 1
                for fc in range(8):
                    ps = PB[2 + fc % 4][:, 0:n]
                    S.mm(ps, [(wu[:, kc, fc * 128:(fc + 1) * 128], A2T[:, kc, g0:g1]) for kc in range(8)])
                    rl = RL[fc % 2]
                    act(rl[:, 0:n], ps, AF.Relu)
                    tt("dve", gt[:, fc, 0:n], rl[:, 0:n], rl[:, 0:n], ALU.mult)
                return gt

            def emit_down(qd, g0, g1, gt):
                wd = WD[qd % 2]
                for ti in range(g0 // 128, g1 // 128):
                    lt = slice(ti * 128 - g0, ti * 128 - g0 + 128)
                    for nn in range(2):
                        ps_h = PB[nn][:, :]
                        S.mm(ps_h, [(gt[:, fc, lt], wd[:, fc, nn * 512:(nn + 1) * 512]) for fc in range(8)])
                        hs = H[:, ti, nn * 512:(nn + 1) * 512]
                        tt("dve", hs, hs, ps_h, ALU.add)

            def emit_group(qd, g0, g1):
                wu, wd = WU[qd % 2], WD[qd % 2]
                n = g1 - g0
                gt = GT[gi_box[0] % 2]
                gi_box[0] += 1
                for fc in range(8):
                    ps = PB[2 + fc % 4][:, 0:n]
                    S.mm(ps, [(wu[:, kc, fc * 128:(fc + 1) * 128], A2T[:, kc, g0:g1]) for kc in range(8)])
                    rl = RL[fc % 2]
                    act(rl[:, 0:n], ps, AF.Relu)
                    tt("dve", gt[:, fc, 0:n], rl[:, 0:n], rl[:, 0:n], ALU.mult)
                for ti in range(g0 // 128, g1 // 128):
                    lt = slice(ti * 128 - g0, ti * 128 - g0 + 128)
                    for nn in range(2):
                        ps_h = PB[nn][:, :]
                        S.mm(ps_h, [(gt[:, fc, lt], wd[:, fc, nn * 512:(nn + 1) * 512]) for fc in range(8)])
                        hs = H[:, ti, nn * 512:(nn + 1) * 512]
                        tt("dve", hs, hs, ps_h, ALU.add)

            load_q(1)
            emit_group(0, *tgroups[0])
            x_stream = S.capture
            S.capture = []
            for t in late_norm:
                norm_b(t)
            y_stream = S.capture
            S.capture = None
            S.run_streams([x_stream, y_stream])

            S.capture = []
            seq = [(qd, gi_, g0, g1) for qd in range(4) for gi_, (g0, g1) in enumerate(tgroups)
                   if not (qd == 0 and gi_ == 0)]
            pend_ = None
            for k_ in range(len(seq) + 1):
                if k_ < len(seq):
                    qd, gi_, g0, g1 = seq[k_]
                    gt_new = emit_up(qd, g0, g1)
                if pend_ is not None:
                    pq, pgi, pg0, pg1, pgt = pend_
                    emit_down(pq, pg0, pg1, pgt)
                    if pgi == len(tgroups) - 1 and pq + 2 < 4:
                        load_q(pq + 2)
                pend_ = (qd, gi_, g0, g1, gt_new) if k_ < len(seq) else None
            b_stream = S.capture
            S.capture = None
            streams_b = [b_stream]
            if has_s:
                S.capture = []
                sample_state_update()
                streams_b.append(S.capture)
                S.capture = None
            S.run_streams(streams_b)

            bgC = load_phaseA_weights(False) if sg_i + 1 < len(SGS) else None
            C = Arena(R2_OFF)
            A3 = [C.bf(DM) for _ in range(2)]
            A3T = [C.bf(8 * 128, ("p (k n) -> p k n", dict(k=8))) for _ in range(2)]
            PF = [C.f32(PLE) for _ in range(2)]
            PBF = [C.bf(PLE) for _ in range(2)]
            PT = [C.bf(2 * 128, ("p (k n) -> p k n", dict(k=2))) for _ in range(2)]
            THC = [C.f32(DM) for _ in range(2)]
            TC = [C.f32(DM) for _ in range(3)]
            YO = [C.f32(DM) for _ in range(2)]
            JK = C.bf(DM)
            assert C.off <= WA_OFF, ("phase C overlaps prefetched weights", C.off, WA_OFF)
            bC0 = Banks([0, 1])
            bC1 = Banks([2, 3, 4, 5])

            def stage_C0(t):
                par = t["slot"] % 2
                d5 = t["slot"] % 5
                hsl = H[:, t["slot"], :]
                S.dma(PF[par], t["p"])
                ss = small(d5, 140, 1); rs = small(d5, 141, 1); tmp = small(d5, 142, 1)
                memset("pool", ss, 0.0)
                act(A3[par], hsl, AF.Square, accum=ss)
                rstd_from_ss(tmp, ss, 1, 1.0 / DM, rs)
                act(A3[par], hsl, AF.Copy, scale=rs)
                cp("pool", PBF[par], PF[par])
                yield

            def stage_C0b(t):
                par = t["slot"] % 2
                bn_ = bC0.next()
                for kc in range(8):
                    S.tr(PBB[bn_][:, kc * 128:(kc + 1) * 128], A3[par][:, kc * 128:(kc + 1) * 128], IDB[:])
                tt("dve", A3T[par], PBB[bn_][:, 0:1024].rearrange("p (k t) -> p k t", k=8),
                   GN[:, GN_PLE:GN_PLE + 8].unsqueeze(2).to_broadcast([128, 8, 128]), ALU.mult)
                bt = bC0.next()
                for c in range(2):
                    S.tr(PBB[bt][:, c * 128:(c + 1) * 128], PBF[par][:, c * 128:(c + 1) * 128], IDB[:])
                cp("act", PT[par], PBB[bt][:, 0:256].rearrange("p (k t) -> p k t", k=2))
                yield

            def stage_C1(t):
                par = t["slot"] % 2
                p3 = t["slot"] % 3
                for nn in range(2):
                    ps_g = PB[bC1.next()][:, :]
                    S.mm(ps_g, [(A3T[par][:, kc, :], WG[:, kc, nn * 512:(nn + 1) * 512]) for kc in range(8)])
                    act(THC[par][:, nn * 512:(nn + 1) * 512], ps_g, AF.Tanh, scale=0.5)
                    ps_p = PB[bC1.next()][:, :]
                    S.mm(ps_p, [(PT[par][:, c, :], WP[:, c, nn * 512:(nn + 1) * 512]) for c in range(2)])
                    stt("dve", TC[p3][:, nn * 512:(nn + 1) * 512], THC[par][:, nn * 512:(nn + 1) * 512], 1.0, ps_p,
                        ALU.add, ALU.mult)
                    yield

            def stage_C2(t):
                p3 = t["slot"] % 3
                d5 = t["slot"] % 5
                hsl = H[:, t["slot"], :]
                stt("dve", hsl, TC[p3], 0.5, hsl, ALU.mult, ALU.add)
                ss = small(d5, 144, 1)
                memset("pool", ss, 0.0)
                act(JK, hsl, AF.Square, accum=ss)
                yield

            def stage_C3(t):
                par = t["slot"] % 2
                d5 = t["slot"] % 5
                hsl = H[:, t["slot"], :]
                ss = small(d5, 144, 1); rs = small(d5, 145, 1); tmp = small(d5, 146, 1)
                rstd_from_ss(tmp, ss, 1, 1.0 / DM, rs)
                stt("dve", YO[par], hsl, rs, NF, ALU.mult, ALU.mult)
                S.dma(t["y"], YO[par])
                yield

            S.marks.append(("B end", sg_i, max(S.free.values())))
            run_pipeline(sg, [stage_C0, stage_C0b, stage_C1, stage_C2, stage_C3], background={0: bgC})
            S.marks.append(("C end", sg_i, max(S.free.values())))

        S.finish()
    return nc, S


_CACHE = {}


def _consts():
    idx = np.arange(128)
    seq = idx // 8
    ident = np.eye(128, dtype=np.float32)
    ones = np.ones((128, 128), np.float32)
    causal_ts = (idx[None, :] <= idx[:, None])
    same = (seq[:, None] == seq[None, :])
    cum_P = causal_ts.T.astype(np.float32)
    cum_S = (causal_ts.T & same).astype(np.float32)
    mask_P = np.where(causal_ts, 0.0, NEG).astype(np.float32)
    mask_S = np.where(causal_ts & same, 0.0, NEG).astype(np.float32)
    maskT_P = mask_P.T.copy()
    maskT_S = mask_S.T.copy()
    sel_P = np.zeros((128, 128), np.float32); sel_P[127, :] = 1.0
    sel_S = (idx[:, None] == (seq[None, :] * 8 + 7)).astype(np.float32)
    t01_P = causal_ts.astype(np.float32)
    bd01_S = (causal_ts & same).astype(np.float32)
    cmat = np.stack([ident, ones, cum_P, cum_S, mask_P, mask_S, maskT_P, maskT_S, sel_P, sel_S, t01_P, bd01_S])
    lastmask = (idx[:, None] == (np.arange(16)[None, :] * 8 + 7)).astype(np.float32)
    seqmask = (seq[:, None] == np.arange(16)[None, :]).astype(np.float32)
    csm = np.concatenate([lastmask, seqmask], axis=1)
    return np.ascontiguousarray(cmat), np.ascontiguousarray(csm)


def kernel(x_prompt, x_sample, p_prompt, p_sample, state_C, state_n, state_m,
           norm_mix, w_in, gm_ln_g, gm_ln_b, gm_ws, gm_bs, ml_b_i, ml_b_f, ml_norm,
           w_out, norm_ffn, w_up, w_down, norm_ple, w_ple_gate, w_ple_proj, norm_final):
    f = lambda a: np.ascontiguousarray(np.asarray(a, dtype=np.float32))
    x_prompt, x_sample, p_prompt, p_sample = f(x_prompt), f(x_sample), f(p_prompt), f(p_sample)
    state_C, state_n, state_m = f(state_C), f(state_n), f(state_m)
    if "nc" not in _CACHE:
        _CACHE["nc"] = build()[0]
    nc = _CACHE["nc"]
    cmat, csm = _consts()
    gbs = f(gm_bs)[0]
    gains = np.concatenate([f(norm_mix)[0].reshape(8, 128).T, f(norm_ffn)[0].reshape(8, 128).T,
                            f(norm_ple)[0].reshape(8, 128).T, gbs.T, np.tile(gbs[:, :8], (1, 16)).T], axis=1)
    rowv = np.concatenate([f(gm_ln_g)[0], f(gm_ln_b)[0], f(ml_b_i)[0], f(ml_b_f)[0]])
    gws = f(gm_ws)[0]
    wsg = np.concatenate([gws, np.tile(gws[:, :8, :8], (1, 16, 16))], axis=0)
    shared = dict(w_in=f(w_in)[0], w_out=f(w_out)[0], w_up=f(w_up)[0], w_down=f(w_down)[0],
                  w_pg=f(w_ple_gate)[0], w_pp=f(w_ple_proj)[0], gains=f(gains), rowv=f(rowv),
                  nfin=f(norm_final), mlg=f(ml_norm)[0], wsg=f(wsg), cmat=cmat, csm=csm)
    in_maps = []
    for c in range(NCORES):
        sl = slice(16 * c, 16 * (c + 1))
        m = dict(shared)
        m.update(xp=x_prompt[c], xs=x_sample[sl].reshape(128, DM), pp=p_prompt[0, c],
                 psm=p_sample[0, sl].reshape(128, PLE), sC=state_C[0, sl],
                 snT=np.ascontiguousarray(state_n[0, sl].transpose(1, 2, 0)),
                 smt=np.ascontiguousarray(np.repeat(state_m[0, sl], 8, axis=0)))
        in_maps.append(m)
    res = run_bass_kernel_spmd(nc, in_maps, core_ids=list(range(NCORES)))
    R = res.results
    g = lambda k: [np.asarray(R[c][k], dtype=np.float32) for c in range(NCORES)]
    y_prompt = np.stack(g("y_p"))
    y_sample = np.concatenate(g("y_s")).reshape(128, 8, DM)
    Cp = np.stack(g("o_Cp"))[None]
    Np_ = np.stack(g("o_Np"))[None]
    Mp = np.stack([a.reshape(4) for a in g("o_Mp")])[None]
    Vp = np.stack(g("o_Vp"))[None]
    Cs = np.concatenate(g("o_Cs"))[None]
    Ns = np.concatenate([a.transpose(2, 0, 1) for a in g("o_NsT")])[None]
    Ms = np.concatenate(g("o_Ms"))[None]
    Vs = np.concatenate(g("o_Vs")).reshape(128, 8, 512)[None]
    return (y_prompt, y_sample, Cp, Np_, Mp, Vp, Cs, Ns, Ms, Vs)
```
